# Optimizing a Trainium2 kernel written in Bass

```python
import math
import jax, jax.numpy as jnp
from jax import lax
import numpy as np


D_MODEL = 2048
BATCH = 2
SEQ = 8192
DEPTH = 1

HEAD_DIM = 128
N_MOBA_HEADS = D_MODEL // (2 * HEAD_DIM)
N_FOX_HEADS = D_MODEL // (2 * HEAD_DIM)
MOBA_WIDTH = N_MOBA_HEADS * HEAD_DIM
FOX_WIDTH = N_FOX_HEADS * HEAD_DIM
MOBA_BLOCK = 256
MOBA_TOP_K = 3
MOBA_Q_CHUNK = 64
FOX_Q_BLOCK = 128
N_MEM = 256
N_CROSS_HEADS = 4
CROSS_WIDTH = N_CROSS_HEADS * HEAD_DIM
D_FF = 4 * D_MODEL
NUM_BUCKETS = 32
MAX_DISTANCE = 1024
RMS_EPS = 1e-6
NEG_INF = -1e30
IN_SPLIT_WIDTHS = (MOBA_WIDTH, MOBA_WIDTH, MOBA_WIDTH, FOX_WIDTH, FOX_WIDTH, FOX_WIDTH, N_FOX_HEADS, D_MODEL, D_MODEL)
IN_WIDTH = sum(IN_SPLIT_WIDTHS)
IN_SPLIT_POINTS = tuple(int(v) for v in np.cumsum(IN_SPLIT_WIDTHS)[:-1])

kernel_name = 'hybrid_moba_fox_gated_block'


def rmsnorm(x, g):
    xf = x.astype(jnp.float32)
    y = xf * lax.rsqrt(jnp.mean(xf * xf, axis=-1, keepdims=True) + RMS_EPS)
    return (y * g.astype(jnp.float32)).astype(x.dtype)


def split_heads(t, n_heads):
    b, s, _ = t.shape
    return t.reshape(b, s, n_heads, -1).transpose(0, 2, 1, 3)


def merge_heads(t):
    b, h, s, d = t.shape
    return t.transpose(0, 2, 1, 3).reshape(b, s, h * d)


def t5_bucket(dist):
    n = jnp.maximum(dist, 0)
    max_exact = NUM_BUCKETS // 2
    nf = jnp.maximum(n, 1).astype(jnp.float32)
    large = max_exact + (jnp.log(nf / max_exact) / math.log(MAX_DISTANCE / max_exact)
                         * (NUM_BUCKETS - max_exact)).astype(jnp.int32)
    large = jnp.minimum(large, NUM_BUCKETS - 1)
    return jnp.where(n < max_exact, n, large)


def moba_attention(q, k, v, rel_bias):
    b, h, s, dh = q.shape
    nb = -(-s // MOBA_BLOCK)
    pad = nb * MOBA_BLOCK - s
    kb = jnp.pad(k, ((0, 0), (0, 0), (0, pad), (0, 0))).reshape(b, h, nb, MOBA_BLOCK, dh)
    vb = jnp.pad(v, ((0, 0), (0, 0), (0, pad), (0, 0))).reshape(b, h, nb, MOBA_BLOCK, dh)
    kbar = jnp.mean(kb.astype(jnp.float32), axis=3)
    top_k = min(MOBA_TOP_K, nb)
    scale = dh ** -0.5
    block_ids = jnp.arange(nb)
    key_off = jnp.arange(MOBA_BLOCK)
    head_idx = jnp.arange(h)[:, None, None, None]
    bias_tab = rel_bias.astype(jnp.float32)
    gather_blocks = jax.vmap(jax.vmap(lambda blocks, idx: blocks[idx]))

    def chunk(ci):
        start = ci * MOBA_Q_CHUNK
        qc = lax.dynamic_slice_in_dim(q, start, MOBA_Q_CHUNK, axis=2)
        t = start + jnp.arange(MOBA_Q_CHUNK)
        own = start // MOBA_BLOCK
        gate = jnp.einsum('bhqd,bhnd->bhqn', qc.astype(jnp.float32), kbar)
        gate = jnp.where(block_ids < own, gate, NEG_INF)
        _, sel = lax.top_k(gate, top_k)
        valid = sel < own
        ks = gather_blocks(kb, sel)
        vs = gather_blocks(vb, sel)
        s_sel = jnp.einsum('bhqd,bhqkld->bhqkl', qc, ks).astype(jnp.float32) * scale
        pos_sel = sel[..., None] * MOBA_BLOCK + key_off
        s_sel = s_sel + bias_tab[head_idx, t5_bucket(t[:, None, None] - pos_sel)]
        s_sel = jnp.where(valid[..., None], s_sel, NEG_INF)
        k_own = lax.dynamic_index_in_dim(kb, own, axis=2, keepdims=False)
        v_own = lax.dynamic_index_in_dim(vb, own, axis=2, keepdims=False)
        pos_own = own * MOBA_BLOCK + key_off
        s_own = jnp.einsum('bhqd,bhld->bhql', qc, k_own).astype(jnp.float32) * scale
        s_own = s_own + bias_tab[:, t5_bucket(t[:, None] - pos_own[None, :])]
        s_own = jnp.where(pos_own[None, :] <= t[:, None], s_own, NEG_INF)
        logits = jnp.concatenate([s_sel.reshape(b, h, MOBA_Q_CHUNK, top_k * MOBA_BLOCK), s_own], axis=-1)
        p = jax.nn.softmax(logits, axis=-1).astype(v.dtype)
        p_sel = p[..., :top_k * MOBA_BLOCK].reshape(b, h, MOBA_Q_CHUNK, top_k, MOBA_BLOCK)
        p_own = p[..., top_k * MOBA_BLOCK:]
        return (jnp.einsum('bhqkl,bhqkld->bhqd', p_sel, vs)
                + jnp.einsum('bhql,bhld->bhqd', p_own, v_own))

    out = lax.map(chunk, jnp.arange(s // MOBA_Q_CHUNK))
    return jnp.moveaxis(out, 0, 2).reshape(b, h, s, dh)


def forgetting_attention(q, k, v, log_f):
    b, h, s, dh = q.shape
    scale = dh ** -0.5
    c = jnp.cumsum(log_f, axis=-1)
    outs = []
    for i in range(s // FOX_Q_BLOCK):
        s0, s1 = i * FOX_Q_BLOCK, (i + 1) * FOX_Q_BLOCK
        logits = jnp.einsum('bhqd,bhkd->bhqk', q[:, :, s0:s1], k[:, :, :s1]).astype(jnp.float32) * scale
        logits = logits + c[:, :, s0:s1, None] - c[:, :, None, :s1]
        mask = jnp.arange(s0, s1)[:, None] >= jnp.arange(s1)[None, :]
        p = jax.nn.softmax(jnp.where(mask, logits, NEG_INF), axis=-1).astype(v.dtype)
        outs.append(jnp.einsum('bhqk,bhkd->bhqd', p, v[:, :, :s1]))
    return jnp.concatenate(outs, axis=2)


def memory_cross_attention(c, m, w_cq, w_ck, w_cv, w_co):
    q = split_heads(c @ w_cq, N_CROSS_HEADS)
    k = split_heads(m @ w_ck, N_CROSS_HEADS)
    v = split_heads(m @ w_cv, N_CROSS_HEADS)
    logits = jnp.einsum('bhqd,bhmd->bhqm', q, k).astype(jnp.float32) * HEAD_DIM ** -0.5
    p = jax.nn.softmax(logits, axis=-1).astype(v.dtype)
    return merge_heads(jnp.einsum('bhqm,bhmd->bhqd', p, v)) @ w_co


def setup_inputs(seed: int = 0) -> dict:
    key = jax.random.key(seed)
    ks = jax.random.split(key, 24)
    f32 = jnp.float32

    def dense(k, shape):
        return jax.random.normal(k, shape, f32) * shape[-2] ** -0.5

    def gain(k, shape):
        return 1.0 + 0.02 * jax.random.normal(k, shape, f32)

    return {
        'x': jax.random.normal(ks[0], (BATCH, SEQ, D_MODEL), f32),
        'mem': jax.random.normal(ks[1], (BATCH, N_MEM, D_MODEL), f32),
        'g_mix': gain(ks[2], (DEPTH, D_MODEL)),
        'w_in': dense(ks[3], (DEPTH, D_MODEL, IN_WIDTH)),
        'b_forget': 3.0 + 0.5 * jax.random.normal(ks[4], (DEPTH, N_FOX_HEADS), f32),
        'w_branch_moba': dense(ks[5], (DEPTH, MOBA_WIDTH, D_MODEL)),
        'w_branch_fox': dense(ks[6], (DEPTH, FOX_WIDTH, D_MODEL)),
        'w_mix_out': dense(ks[7], (DEPTH, D_MODEL, D_MODEL)),
        'rel_bias': 0.5 * jax.random.normal(ks[8], (N_MOBA_HEADS, NUM_BUCKETS), f32),
        'g_cross': gain(ks[9], (DEPTH, D_MODEL)),
        'g_mem': gain(ks[10], (DEPTH, D_MODEL)),
        'w_cq': dense(ks[11], (DEPTH, D_MODEL, CROSS_WIDTH)),
        'w_ck': dense(ks[12], (DEPTH, D_MODEL, CROSS_WIDTH)),
        'w_cv': dense(ks[13], (DEPTH, D_MODEL, CROSS_WIDTH)),
        'w_co': dense(ks[14], (DEPTH, CROSS_WIDTH, D_MODEL)),
        'g_mlp': gain(ks[15], (DEPTH, D_MODEL)),
        'w_ff1': dense(ks[16], (DEPTH, D_MODEL, D_FF)),
        'w_ff2': dense(ks[17], (DEPTH, D_FF, D_MODEL)),
        'g_final': gain(ks[18], (D_MODEL,)),
    }


def reference(x, mem, g_mix, w_in, b_forget, w_branch_moba, w_branch_fox, w_mix_out, rel_bias,
              g_cross, g_mem, w_cq, w_ck, w_cv, w_co, g_mlp, w_ff1, w_ff2, g_final):
    h = x
    for l in range(DEPTH):
        a = rmsnorm(h, g_mix[l])
        z = a @ w_in[l]
        q_a, k_a, v_a, q_f, k_f, v_f, f_logit, gate_a, gate_f = jnp.split(z, IN_SPLIT_POINTS, axis=-1)
        o_a = merge_heads(moba_attention(split_heads(q_a, N_MOBA_HEADS), split_heads(k_a, N_MOBA_HEADS),
                                         split_heads(v_a, N_MOBA_HEADS), rel_bias))
        log_f = jax.nn.log_sigmoid((f_logit + b_forget[l]).astype(jnp.float32)).transpose(0, 2, 1)
        o_f = merge_heads(forgetting_attention(split_heads(q_f, N_FOX_HEADS), split_heads(k_f, N_FOX_HEADS),
                                               split_heads(v_f, N_FOX_HEADS), log_f))
        merged = (jax.nn.sigmoid(gate_a) * (o_a @ w_branch_moba[l])
                  + jax.nn.sigmoid(gate_f) * (o_f @ w_branch_fox[l]))
        h = h + merged @ w_mix_out[l]
        h = h + memory_cross_attention(rmsnorm(h, g_cross[l]), rmsnorm(mem, g_mem[l]),
                                       w_cq[l], w_ck[l], w_cv[l], w_co[l])
        u = rmsnorm(h, g_mlp[l]) @ w_ff1[l]
        h = h + jnp.square(jax.nn.relu(u)) @ w_ff2[l]
    return rmsnorm(h, g_final)
```

```python
import math
import os
from contextlib import ExitStack
import numpy as np
import concourse.bass as bass
import concourse.mybir as mybir
from concourse.bass_utils import run_bass_kernel_spmd

F32 = mybir.dt.float32
BF16 = mybir.dt.bfloat16
U8 = mybir.dt.uint8
AF = mybir.ActivationFunctionType
ALU = mybir.AluOpType
AX = mybir.AxisListType

ENGS = ("pe", "act", "dve", "pool", "sp")

D = 2048
DC = 16
NH = 8
C_QA, C_KA, C_VA, C_QF, C_KF, C_VF, C_FL, C_GA, C_GF = 0, 1024, 2048, 3072, 4096, 5120, 6144, 6152, 8200
IN_W = 10248
SCALE = 128 ** -0.5
EPS = 1e-6
NEG = -30000.0


class Buf:
    __slots__ = ("writers", "readers")

    def __init__(self):
        self.writers = []
        self.readers = {}


class Ins:
    __slots__ = ("eng", "fn", "deps", "is_dma", "marked", "sem", "val")

    def __init__(self, eng, fn, is_dma):
        self.eng = eng
        self.fn = fn
        self.deps = []
        self.is_dma = is_dma
        self.marked = False
        self.sem = None
        self.val = 0


class Prog:
    N_DMA_SEMS = 14

    def __init__(self, nc):
        self.nc = nc
        self.q = {e: [] for e in ENGS}
        self.last = {e: None for e in ENGS}
        self.dmas = []

    def add(self, eng, fn, r=(), w=(), is_dma=False, extra=()):
        ins = Ins(eng, fn, is_dma)
        deps = list(extra)
        for b in r:
            deps.extend(b.writers)
        for b in w:
            deps.extend(b.writers)
            deps.extend(b.readers.values())
        seen = set()
        for d in deps:
            if d is ins or id(d) in seen:
                continue
            seen.add(id(d))
            if (not d.is_dma) and (not is_dma) and d.eng == eng and eng == "pe":
                continue
            ins.deps.append(d)
            d.marked = True
        for b in r:
            key = ("dma", id(ins)) if is_dma else eng
            b.readers[key] = ins
        for b in w:
            b.writers = [ins]
            b.readers = {}
        self.q[eng].append(ins)
        if is_dma:
            self.dmas.append(ins)
        else:
            self.last[eng] = ins
        return ins

    def barrier(self):
        nc = self.nc
        eo = {"pe": nc.tensor, "act": nc.scalar, "dve": nc.vector, "pool": nc.gpsimd, "sp": nc.sync}
        lasts = [v for v in self.last.values() if v is not None]
        dm = list(self.dmas)
        self.dmas = []
        for e in ENGS:
            self.add(e, (lambda o=eo[e]: o.nop()), extra=[x for x in lasts if x.eng != e] + dm)

    def emit(self, block, sems):
        for e in ENGS:
            cnt = 0
            dma_cnt = [0] * self.N_DMA_SEMS
            last_on = [None] * self.N_DMA_SEMS
            rr = 0
            for ins in self.q[e]:
                if ins.is_dma:
                    k = rr % self.N_DMA_SEMS
                    rr += 1
                    dma_cnt[k] += 16
                    ins.sem = sems["dma_" + e][k]
                    ins.val = dma_cnt[k]
                    if last_on[k] is not None:
                        ins.deps.append(last_on[k])
                    last_on[k] = ins
                elif ins.marked:
                    cnt += 1
                    ins.sem = sems[e]
                    ins.val = cnt
            if os.environ.get("K_STATS"):
                print("STATS", e, "n_ins", len(self.q[e]), "marked", cnt, "dma_max", max(dma_cnt), flush=True)
        nc = self.nc
        engobj = {"pe": nc.tensor, "act": nc.scalar, "dve": nc.vector, "pool": nc.gpsimd, "sp": nc.sync}

        def run(e):
            def body(_eng):
                eo = engobj[e]
                waited = {}
                for ins in self.q[e]:
                    need = {}
                    for d in ins.deps:
                        s = d.sem
                        if d.val > need.get(id(s), (None, 0))[1]:
                            need[id(s)] = (s, d.val)
                    for sid, (s, v) in need.items():
                        if waited.get(sid, 0) < v:
                            eo.wait_ge(s, v)
                            waited[sid] = v
                    bi = ins.fn()
                    if ins.is_dma:
                        bi.then_inc(ins.sem, 16)
                    elif ins.marked:
                        bi.then_inc(ins.sem, 1)
            return body

        block.tensor(run("pe"))
        block.scalar(run("act"))
        block.vector(run("dve"))
        block.gpsimd(run("pool"))
        block.sync(run("sp"))


class T:
    __slots__ = ("v", "b")

    def __init__(self, v):
        self.v = v
        self.b = Buf()


class Arena:
    def __init__(self, t, size):
        self.t = t
        self.size = size
        self.off = 0

    def alloc(self, shape, dt):
        esz = 4 if dt == F32 else 2
        n = int(np.prod(shape[1:])) * esz
        assert self.off + n <= self.size, ("sbuf arena overflow", self.off, n, self.size)
        v = self.t[0:shape[0], self.off:self.off + n].bitcast(dt)
        self.off += (n + 63) // 64 * 64
        if len(shape) == 3:
            v = v.rearrange("p (a b) -> p a b", a=shape[1])
        elif len(shape) == 4:
            v = v.rearrange("p (a b c) -> p a b c", a=shape[1], b=shape[2])
        return T(v)

    def ring(self, n, shape, dt):
        return [self.alloc(shape, dt) for _ in range(n)]


ARENA_BYTES = 212800


def build(S, DFF, dbg=False, stop=None):
    NB = S // 256
    NBO = NB // 4
    NO = NBO * 256
    NT = S // 128
    NFG = DFF // 2048
    nc = bass.Bass("TRN2", target_bir_lowering=False)

    def din(name, shape):
        return nc.dram_tensor(name, list(shape), F32, kind="ExternalInput").ap()

    kscr = "ExternalOutput" if dbg else "Internal"

    def dscr(name, shape, dt=BF16):
        return nc.dram_tensor(name, list(shape), dt, kind=kscr)

    xb = din("xb", [S, D]); xbr = din("xbr", [S, D]); xo = din("xo", [NO, D]); memb = din("memb", [256, D])
    w_in = din("w_in", [D, IN_W]); w_bm = din("w_bm", [1024, D]); w_bf = din("w_bf", [1024, D])
    w_mix = din("w_mix", [D, D]); w_cq = din("w_cq", [D, 512]); w_ck = din("w_ck", [D, 512])
    w_cv = din("w_cv", [D, 512]); w_co = din("w_co", [512, D]); w_ff1 = din("w_ff1", [D, DFF])
    w_ff2 = din("w_ff2", [DFF, D])
    gmix_d = din("g_mix_b", [128, D]); gcross_d = din("g_cross_b", [128, D]); gmem_d = din("g_mem_b", [128, D])
    gmlp_d = din("g_mlp_b", [128, D]); gfin_d = din("g_final_b", [128, D])
    bf_d = din("b_forget_c", [8, 1]); rbT_d = din("rbT", [32, 8]); rb31_d = din("rb31b", [128, 8])
    oh_d = din("oh33", [33, 4096]); fmask_d = din("fmask", [128, 2048]); sel_d = din("sel", [128, 1024])
    elneg_d = din("eligneg", [128, NBO * NB]); el01_d = din("elig01", [128, NBO * NB])
    own01_d = din("own01", [128, NBO * NB]); ident_d = din("ident", [128, 128])
    out_d = nc.dram_tensor("out", [NO, D], F32, kind="ExternalOutput").ap()

    KTf = dscr("KTf", [8, 128, S]); KTa = dscr("KTa", [8, 128, S])
    Vf = dscr("Vf", [8, 128, NT, 129]); Va = dscr("Va", [8, 128, NT, 129])
    QTs = dscr("QTs", [16, 128, NO])
    OTs = dscr("OTs", [16, 128, NO])
    Fs = dscr("Fs", [8, 4096])

    es = ExitStack()
    with es:
        arena_t = es.enter_context(nc.sbuf_tensor("arena", [128, ARENA_BYTES], U8))
        A = Arena(arena_t, ARENA_BYTES)
        pbank = [es.enter_context(nc.psum_tensor(f"pb{i}", [128, 512], F32)) for i in range(8)]
        sems = {}
        for e in ENGS:
            sems[e] = es.enter_context(nc.semaphore("s_" + e))
        for e in ("sp", "pool", "act"):
            sems["dma_" + e] = [es.enter_context(nc.semaphore(f"d_{e}{k}")) for k in range(Prog.N_DMA_SEMS)]
        block = es.enter_context(nc.Block())
        P = Prog(nc)

        def bufs(ts):
            return [t.b for t in ts]

        def dma(eng, out, in_, r=(), w=()):
            eo = {"sp": nc.sync, "pool": nc.gpsimd, "act": nc.scalar}[eng]
            return P.add(eng, lambda: eo.dma_start(out=out, in_=in_), r=bufs(r), w=bufs(w), is_dma=True)

        def mm(out, lhsT, rhs, start, stop, r=(), w=()):
            return P.add("pe", lambda: nc.tensor.matmul(out, lhsT=lhsT, rhs=rhs, start=start, stop=stop),
                         r=bufs(r), w=bufs(w))

        def tr(out, in_, ident, r=(), w=()):
            return P.add("pe", lambda: nc.tensor.transpose(out=out, in_=in_, identity=ident), r=bufs(r), w=bufs(w))

        def act(out, in_, func, r=(), w=(), **kw):
            return P.add("act", lambda: nc.scalar.activation(out=out, in_=in_, func=func, **kw), r=bufs(r), w=bufs(w))

        def vec(eng, name, r=(), w=(), **kw):
            eo = {"dve": nc.vector, "pool": nc.gpsimd}[eng]
            return P.add(eng, lambda: getattr(eo, name)(**kw), r=bufs(r), w=bufs(w))

        def copy_any(i, out, in_, r=(), w=()):
            if i % 2 == 0:
                return act(out, in_, AF.Copy, r=r, w=w)
            return vec("dve", "tensor_copy", r=r, w=w, out=out, in_=in_)

        def dbg_dump(name, t, shape):
            if not dbg:
                return
            dd = nc.dram_tensor("dbg_" + name, list(shape), F32, kind="ExternalOutput").ap()
            if len(shape) == 3:
                dma("sp", dd[:, :, :], t.v, r=[t])
            elif len(shape) == 4:
                dma("sp", dd[:, :, :, :], t.v, r=[t])
            else:
                dma("sp", dd[:, :], t.v, r=[t])

        class PS:
            pass
        mmp = [T(pbank[i][:, :]) for i in range(4)]
        trp = [T(pbank[4 + i][:, :].bitcast(BF16)) for i in range(2)]
        misc = [T(pbank[6 + i][:, :]) for i in range(2)]
        stp = [T(pbank[i][:, 0:256]) for i in range(2)]
        for i in range(2):
            t_ = T(pbank[4 + i][:, 0:256]); t_.b = trp[i].b
            stp.append(t_)
        onp = []
        for bk in range(2):
            pair = [T(pbank[2 + bk][:, sub * 256:sub * 256 + 129]) for sub in range(2)]
            pair[1].b = pair[0].b
            onp.append(pair)
        fop = [T(pbank[6][:, 0:129]), T(pbank[7][:, 0:129])]
        cnt = {"mm": 0, "tr": 0, "misc": 0, "st": 0, "on": 0, "ev": 0}

        def nxt(pool, key):
            i = cnt[key]
            cnt[key] += 1
            return pool[i % len(pool)]

        identb = A.alloc([128, 128], BF16); ident32 = A.alloc([128, 128], F32)
        negc = A.alloc([128, NT, 8], F32)
        kbarT = A.alloc([128, 8, NB], F32)
        elneg = A.alloc([128, NBO * NB], F32); el01 = A.alloc([128, NBO * NB], F32); own01 = A.alloc([128, NBO * NB], F32)
        rb31 = A.alloc([128, 8], F32)
        fmask = A.alloc([128, 8, 256], BF16)
        KcT = A.alloc([128, 4, 256], BF16); Vc = A.alloc([128, 2, 4, 129], BF16)
        znb = A.alloc([128, NBO, 8], F32)
        negb = A.alloc([8, 1], F32); ones8 = A.alloc([8, 512], F32)
        selt = A.alloc([128, 8, 128], F32)
        ssr = A.ring(4, [128, 1], F32); rstdr = A.ring(4, [128, 1], F32)
        sqj = A.alloc([128, 2048], BF16)
        gbr = A.ring(1, [128, 2048], F32)
        atm = A.ring(2, [128, 2048], BF16)
        pers_mark = A.off
        rc = {"ss": 0, "atm": 0, "xin": 0}

        dma("sp", ident32.v, ident_d[:, :], w=[ident32])
        vec("dve", "tensor_copy", r=[ident32], w=[identb], out=identb.v, in_=ident32.v)
        dma("sp", elneg.v, elneg_d[:, :], w=[elneg]); dma("sp", el01.v, el01_d[:, :], w=[el01])
        dma("sp", own01.v, own01_d[:, :], w=[own01]); dma("sp", rb31.v, rb31_d[:, :], w=[rb31])
        dma("pool", fmask.v, fmask_d.rearrange("p (a b) -> p a b", a=8), w=[fmask])
        dma("sp", selt.v, sel_d.rearrange("p (a b) -> p a b", a=8), w=[selt])
        dma("sp", negb.v, bf_d[:, :], w=[negb])
        vec("dve", "tensor_scalar", r=[negb], w=[negb], out=negb.v, in0=negb.v, scalar1=-1.0, scalar2=None, op0=ALU.mult)
        vec("pool", "memset", w=[ones8], ap=ones8.v, constant=1.0)

        def norm_a(xt, gb, ring=None):
            ring = ring or atm
            ss = ssr[rc["ss"] % 4]; rstd = rstdr[rc["ss"] % 4]; rc["ss"] += 1
            a = ring[rc["atm"] % len(ring)]; rc["atm"] += 1
            vec("pool", "memset", w=[ss], ap=ss.v, constant=0.0)
            act(sqj.v, xt.v, AF.Square, r=[xt], w=[sqj, ss], accum_out=ss.v)
            act(rstd.v, ss.v, AF.Sqrt, r=[ss], w=[rstd], bias=EPS, scale=1.0 / D)
            vec("dve", "reciprocal", r=[rstd], w=[rstd], out=rstd.v, in_=rstd.v)
            vec("dve", "scalar_tensor_tensor", r=[xt, rstd, gb], w=[a], out=a.v, in0=xt.v, scalar=rstd.v[:, 0:1],
                in1=gb.v, op0=ALU.mult, op1=ALU.mult)
            return a

        def norm_tr(a, dst, c0):
            for g8 in range(2):
                pt = nxt(trp, "tr")
                for k in range(8):
                    dc = g8 * 8 + k
                    tr(pt.v[:, k * 128:(k + 1) * 128], a.v[:, dc * 128:(dc + 1) * 128], identb.v, r=[a, identb], w=[pt])
                copy_any(cnt["ev"], dst.v[:, g8 * 8:(g8 + 1) * 8, c0:c0 + 128],
                         pt.v.rearrange("p (a b) -> p a b", a=8), r=[pt], w=[dst])
                cnt["ev"] += 1

        def norm_T(xt, gb, dst, c0):
            norm_tr(norm_a(xt, gb), dst, c0)

        def load_w(dst, wd, r0, kc, c0, ncols, k0=0, col_off=0):
            src = wd[r0:r0 + kc * 128, c0:c0 + ncols].rearrange("(c p) n -> p c n", p=128)
            step = max(1, min(kc, 4096 // ncols))
            for k in range(0, kc, step):
                kk = min(step, kc - k)
                dma("pool", dst.v[:, k0 + k:k0 + k + kk, col_off:col_off + ncols], src[:, k:k + kk, :], w=[dst])

        m0 = A.off
        rbx = A.alloc([33, 8], F32); oht = A.alloc([33, 4096], F32); fst = A.alloc([8, 4096], BF16)
        vec("pool", "memset", w=[rbx], ap=rbx.v[32:33, :], constant=NEG)
        dma("sp", rbx.v[0:32, :], rbT_d[:, :], w=[rbx])
        dma("sp", oht.v, oh_d[:, :], w=[oht])
        for k in range(8):
            pm = nxt(misc, "misc")
            mm(pm.v[0:8, :], rbx.v, oht.v[:, k * 512:(k + 1) * 512], True, True, r=[rbx, oht], w=[pm])
            act(fst.v[:, k * 512:(k + 1) * 512], pm.v[0:8, :], AF.Copy, r=[pm], w=[fst], scale=1.0 / SCALE)
        Fs_t = T(None)
        dma("sp", Fs.ap()[:, :], fst.v, r=[fst], w=[Fs_t])
        P.barrier()
        A.off = m0
        if stop == "t5":
            P.emit(block, sems)
            return nc

        def phase_A(moba):
            m0 = A.off
            W = A.alloc([128, 16, 2048], BF16)
            Wf = A.alloc([128, 16, 8], BF16)
            xin = A.ring(3, [128, 2048], F32)
            aT = A.ring(2, [128, 16, 512], BF16)
            kst = A.ring(4, [128, 512], BF16)
            vst = A.ring(2, [128, 8, 129], BF16)
            et = A.alloc([8, 512], F32); lt = A.alloc([8, 512], F32); nct = A.ring(2, [8, 512], F32)
            gb = gbr[0]
            dma("sp", gb.v, gmix_d[:, :], w=[gb])
            load_w(W, w_in, 0, 16, C_KA if moba else C_KF, 1024)
            load_w(W, w_in, 0, 16, C_VA if moba else C_VF, 1024, col_off=1024)
            if not moba:
                load_w(Wf, w_in, 0, 16, C_FL, 8)
            for v in vst:
                vec("pool", "memset", w=[v], ap=v.v, constant=1.0)
            if moba:
                vec("pool", "memset", w=[kbarT], ap=kbarT.v, constant=0.0)
            xsrc = xbr if moba else xb
            KT = KTa if moba else KTf
            VS = Va if moba else Vf
            VSv = VS.ap().rearrange("h p t c -> p h t c")
            atmA = A.ring(5, [128, 2048], BF16)

            def do_norm_a(t):
                outs = []
                for j in range(4):
                    xt = xin[rc["xin"] % 3]; rc["xin"] += 1
                    r0 = t * 512 + j * 128
                    dma("sp", xt.v, xsrc[r0:r0 + 128, :], w=[xt])
                    outs.append(norm_a(xt, gb, atmA))
                return outs

            def do_norm_tr(t, outs):
                for j in range(4):
                    norm_tr(outs[j], aT[t % 2], j * 128)

            do_norm_tr(0, do_norm_a(0))
            for t in range(S // 512):
                a_t = aT[t % 2]
                nxt_a = do_norm_a(t + 1) if t + 1 < S // 512 else None
                for h in range(8):
                    pm = nxt(mmp, "mm")
                    for dc in range(16):
                        mm(pm.v, W.v[:, dc, h * 128:(h + 1) * 128], a_t.v[:, dc, :], dc == 0, dc == 15,
                           r=[W, a_t], w=[pm])
                    ks = kst[(t * 8 + h) % 4]
                    if moba:
                        for i2 in range(2):
                            act(ks.v[:, i2 * 256:(i2 + 1) * 256], pm.v[:, i2 * 256:(i2 + 1) * 256], AF.Copy, r=[pm], w=[ks, kbarT],
                                accum_out=kbarT.v[:, h, 2 * t + i2:2 * t + i2 + 1])
                    else:
                        copy_any(cnt["ev"], ks.v, pm.v, r=[pm], w=[ks]); cnt["ev"] += 1
                    dma("sp", KT.ap()[h, :, t * 512:(t + 1) * 512], ks.v, r=[ks])
                if nxt_a is not None:
                    do_norm_tr(t + 1, nxt_a)
                for j in range(4):
                    vs = vst[(t * 4 + j) % 2]
                    for g in range(2):
                        pm = nxt(mmp, "mm")
                        for dc in range(16):
                            mm(pm.v, a_t.v[:, dc, j * 128:(j + 1) * 128], W.v[:, dc, 1024 + g * 512:1024 + (g + 1) * 512],
                               dc == 0, dc == 15, r=[W, a_t], w=[pm])
                        copy_any(cnt["ev"], vs.v[:, g * 4:(g + 1) * 4, 0:128], pm.v.rearrange("p (a b) -> p a b", a=4),
                                 r=[pm], w=[vs]); cnt["ev"] += 1
                    dma("sp", VSv[:, :, t * 4 + j, :], vs.v, r=[vs])
                if not moba:
                    pm = nxt(misc, "misc")
                    for dc in range(16):
                        mm(pm.v[0:8, :], Wf.v[:, dc, :], a_t.v[:, dc, :], dc == 0, dc == 15, r=[Wf, a_t], w=[pm])
                    act(et.v, pm.v[0:8, :], AF.Exp, r=[pm, negb], w=[et], bias=negb.v[:, 0:1], scale=-1.0)
                    act(lt.v, et.v, AF.Ln, r=[et], w=[lt], bias=1.0)
                    ncur = nct[t % 2]; nprev = nct[(t + 1) % 2]
                    if t == 0:
                        vec("dve", "tensor_tensor_scan", r=[ones8, lt], w=[ncur], out=ncur.v, data0=ones8.v, data1=lt.v,
                            initial=0.0, op0=ALU.mult, op1=ALU.add)
                    else:
                        vec("dve", "tensor_tensor_scan", r=[ones8, lt, nprev], w=[ncur], out=ncur.v, data0=ones8.v,
                            data1=lt.v, initial=nprev.v[:, 511:512], op0=ALU.mult, op1=ALU.add)
                    pm2 = nxt(misc, "misc")
                    for j in range(4):
                        tr(pm2.v[:, j * 8:(j + 1) * 8], ncur.v[:, j * 128:(j + 1) * 128], ident32.v[0:8, 0:8],
                           r=[ncur, ident32], w=[pm2])
                    vec("dve", "tensor_copy", r=[pm2], w=[negc], out=negc.v[:, 4 * t:4 * t + 4, :],
                        in_=pm2.v[:, 0:32].rearrange("p (a b) -> p a b", a=4))
            P.barrier()
            A.off = m0

        phase_A(False)
        if stop == "A1":
            P.emit(block, sems)
            return nc
        phase_A(True)
        if stop == "A2":
            P.emit(block, sems)
            return nc

        mq = A.off
        maskg = A.alloc([128, NO // 128, 8, NB], F32)
        mq2 = A.off
        W = A.alloc([128, 16, 2048], BF16)
        xin = A.ring(3, [128, 2048], F32)
        aT = A.ring(2, [128, 16, 256], BF16)
        qst = A.ring(2, [128, 16, 256], BF16)
        q32 = A.ring(2, [128, 256], F32)
        gt = A.ring(2, [128, 8, NB], F32)
        top8 = A.ring(2, [128, 8], F32)
        atmQ = A.ring(4, [128, 2048], BF16)
        gb = gbr[0]
        dma("sp", gb.v, gmix_d[:, :], w=[gb])
        load_w(W, w_in, 0, 16, C_QA, 1024)
        load_w(W, w_in, 0, 16, C_QF, 1024, col_off=1024)
        QCUT = int(os.environ.get("Q_CUT", "99"))
        QGM = int(os.environ.get('Q_GM', '1'))
        for m in range(min(NBO, int(os.environ.get('Q_SLOTS', '99'))) if QCUT >= 2 else 0):
            a_t = aT[m % 2]; qs = qst[m % 2]

            def do_norm_q_a(mm_):
                outs = []
                for j in range(2):
                    xt = xin[rc["xin"] % 3]; rc["xin"] += 1
                    r0 = mm_ * 256 + j * 128
                    dma("sp", xt.v, xo[r0:r0 + 128, :], w=[xt])
                    outs.append(norm_a(xt, gb, atmQ))
                return outs

            def do_norm_q_tr(mm_, outs):
                for j in range(2):
                    norm_tr(outs[j], aT[mm_ % 2], j * 128)

            if m == 0:
                do_norm_q_tr(0, do_norm_q_a(0))
            nxt_q = do_norm_q_a(m + 1) if m + 1 < NBO else None
            pg = [nxt(misc, "misc"), nxt(misc, "misc")]
            for h in range(16 if QCUT >= 3 else 0):
                pm = nxt(mmp, "mm")
                for dc in range(16):
                    mm(pm.v[:, 0:256], W.v[:, dc, h * 128:(h + 1) * 128], a_t.v[:, dc, :], dc == 0, dc == 15,
                       r=[W, a_t], w=[pm])
                copy_any(cnt["ev"], qs.v[:, h, :], pm.v[:, 0:256], r=[pm], w=[qs]); cnt["ev"] += 1
                if h == 7 and nxt_q is not None:
                    do_norm_q_tr(m + 1, nxt_q)
                if h < 8 and QCUT >= 4:
                    qq = q32[h % 2]
                    QQ = int(os.environ.get('Q_QQ', '6'))
                    if QQ == 0:
                        vec("dve", "tensor_copy", r=[pm], w=[qq], out=qq.v, in_=pm.v[:, 0:256])
                    elif QQ == 1:
                        act(qq.v, pm.v[:, 0:256], AF.Copy, r=[pm], w=[qq])
                    elif QQ == 2:
                        vec("dve", "tensor_copy", r=[qs], w=[qq], out=qq.v, in_=qs.v[:, h, :])
                    elif QQ == 6:
                        vec("dve", "tensor_copy", r=[pm, qs], w=[qq], out=qq.v, in_=pm.v[:, 0:256])
                    elif QQ == 3:
                        vec("pool", "tensor_copy", r=[qs], w=[qq], out=qq.v, in_=qs.v[:, h, :])
                    for j in range(2 if QGM else 0):
                        rhs_ = kbarT.v[:, h, :] if QGM == 1 else ident32.v[:, 0:NB]
                        mm(pg[j].v[:, h * NB:(h + 1) * NB], qq.v[:, j * 128:(j + 1) * 128], rhs_, True, True,
                           r=[qq, kbarT, ident32], w=[pg[j]])
            if QCUT < 5:
                continue
            dma("sp", QTs.ap().rearrange("h p t -> p h t")[:, :, m * 256:(m + 1) * 256], qs.v, r=[qs])
            for j in (range(2) if not os.environ.get('Q_NO_GATE') else ()):
                g = gt[j]
                sub = m * 2 + j
                for h in range(8):
                    vec("dve", "tensor_tensor", r=[pg[j], elneg], w=[g], out=g.v[:, h, :], in0=pg[j].v[:, h * NB:(h + 1) * NB],
                        in1=elneg.v[:, m * NB:(m + 1) * NB], op=ALU.add)
                for h in range(8):
                    t8 = top8[h % 2]
                    vec("dve", "max", r=[g], w=[t8], out=t8.v, in_=g.v[:, h, :])
                    vec("dve", "scalar_tensor_tensor", r=[g, t8, el01], w=[maskg], out=maskg.v[:, sub, h, :], in0=g.v[:, h, :],
                        scalar=t8.v[:, 2:3], in1=el01.v[:, m * NB:(m + 1) * NB], op0=ALU.is_ge, op1=ALU.mult)
                    vec("dve", "tensor_tensor", r=[maskg, own01], w=[maskg], out=maskg.v[:, sub, h, :],
                        in0=maskg.v[:, sub, h, :], in1=own01.v[:, m * NB:(m + 1) * NB], op=ALU.add)
            pz = nxt(misc, "misc")
            for k in (range(8) if not os.environ.get('Q_NO_SEL') else ()):
                mm(pz.v[:, 0:8], selt.v[:, k, :], negc.v[:, 8 * m + k, :], k == 0, k == 7, r=[selt, negc], w=[pz])
            if not os.environ.get('Q_NO_SEL'):
                vec("dve", "tensor_copy", r=[pz], w=[znb], out=znb.v[:, m, :], in_=pz.v[:, 0:8])
        dbg_dump("maskg", maskg, [128, NO // 128, 8, NB])
        dbg_dump("kbarT", kbarT, [128, 8, NB])
        dbg_dump("negc", negc, [128, NT, 8])
        dbg_dump("znb", znb, [128, NBO, 8])
        P.barrier()
        A.off = mq2
        if stop == "Q":
            P.emit(block, sems)
            return nc

        NBmax = NB
        qtr = A.ring(2, [128, NO], BF16)
        ktr = A.ring(2, [128, NBmax * 256], BF16)
        vtr = A.ring(2, [128, 2 * NBmax, 129], BF16)
        bmr = A.ring(2, [128, 16, 256], BF16)
        ptr_ = A.ring(8, [128, 256], BF16)
        oacc = A.ring(2, [128, 2, 129], F32)
        badj = A.ring(2, [128, 2 * NBmax], F32)
        rcp = A.ring(2, [128, 2], F32)
        ohs = A.ring(2, [128, 2, 128], BF16)
        ost = A.ring(2, [128, 256], BF16)
        OTv = OTs.ap().rearrange("h p t -> p h t")
        stpB = stp[0:3]
        trB = trp[1]
        heads = [(k_, h_) for h_ in range(8) for k_ in ("a", "f")]
        loaded = {}
        hk = [0]; pc = [0]; ac = [0]; oc_ = [0]

        def load_head(idx):
            kind, h = heads[idx]
            sl = hk[0] % 2; hk[0] += 1
            k_t, v_t, q_t = ktr[sl], vtr[sl], qtr[sl]
            KT = KTa if kind == "a" else KTf
            VS = Va if kind == "a" else Vf
            hq = h if kind == "a" else 8 + h
            dma("sp", q_t.v, QTs.ap()[hq, :, :], w=[q_t])
            dma("sp", k_t.v, KT.ap()[h, :, :], w=[k_t])
            dma("sp", v_t.v, VS.ap()[h, :, :, :], w=[v_t])
            b_t = None
            if kind == "a":
                b_t = bmr[h % 2]
                for ni in range(8):
                    for half in range(2):
                        src = bass.AP(tensor=Fs, offset=h * 4096 + ni * 512 + 128 * (1 - half), ap=[[1, 128], [1, 256]])
                        dma("sp", b_t.v[:, ni * 2 + half, :], src, r=[Fs_t], w=[b_t])
            loaded[idx] = (k_t, v_t, b_t, q_t)

        load_head(0)
        load_head(1)
        items = []
        for idx, (kind, h) in enumerate(heads):
            for m in range(NBO):
                for st in range(2 * (4 * m + 4)):
                    items.append((idx, kind, h, m, st))
        state = {}

        def group_begin(idx, kind, h, m):
            ext = 4 * m + 4
            i2 = ac[0] % 2; ac[0] += 1
            if kind == "f":
                bj = badj[i2]
                vec("dve", "tensor_scalar", r=[negc, znb], w=[bj], out=bj.v[:, 0:2 * ext], in0=negc.v[:, 0:2 * ext, h],
                    scalar1=znb.v[:, m, h:h + 1], scalar2=None, op0=ALU.subtract)
                state[(idx, m)] = {"bj": bj, "fo": fop}
            else:
                state[(idx, m)] = {"oa": oacc[i2]}

        def qk(it):
            idx, kind, h, m, st = it
            k_t, v_t, b_t, q_t = loaded[idx]
            n = st // 2; half = st % 2
            ps = nxt(stpB, "st")
            extra_mm = None
            if kind == "a":
                ni = n - 4 * m + 4
                if ni >= 0:
                    extra_mm = (b_t.v[:, ni * 2 + half, :], b_t)
            else:
                ni = n - 4 * m
                if ni >= 0:
                    extra_mm = (fmask.v[:, ni * 2 + half, :], fmask)
            mm(ps.v, k_t.v[:, st * 128:(st + 1) * 128], q_t.v[:, m * 256:(m + 1) * 256], True, extra_mm is None,
               r=[k_t, q_t], w=[ps])
            if extra_mm is not None:
                mm(ps.v, identb.v, extra_mm[0], False, True, r=[identb, extra_mm[1]], w=[ps])
            p_t = ptr_[pc[0] % 8]; pc[0] += 1
            if kind == "a":
                if extra_mm is None:
                    act(p_t.v, ps.v, AF.Exp, r=[ps, rb31], w=[p_t], bias=rb31.v[:, h:h + 1], scale=SCALE)
                else:
                    act(p_t.v, ps.v, AF.Exp, r=[ps], w=[p_t], scale=SCALE)
            else:
                bj = state[(idx, m)]["bj"]
                act(p_t.v, ps.v, AF.Exp, r=[ps, bj], w=[p_t], bias=bj.v[:, st:st + 1], scale=SCALE)
            return p_t

        def finish(idx, kind, h, m, oh):
            hq = h if kind == "a" else 8 + h
            for sub in range(2):
                tr(trB.v[:, sub * 128:(sub + 1) * 128], oh.v[:, sub, :], identb.v, r=[oh, identb], w=[trB])
            os_ = ost[oc_[0] % 2]
            act(os_.v, trB.v[:, 0:256], AF.Copy, r=[trB], w=[os_])
            dma("sp", OTs.ap()[hq, :, m * 256:(m + 1) * 256], os_.v, r=[os_])

        def pv(it, p_t):
            idx, kind, h, m, st = it
            k_t, v_t, b_t, q_t = loaded[idx]
            ext = 4 * m + 4
            n = st // 2; half = st % 2
            last = st == 2 * ext - 1
            stt = state[(idx, m)]
            if kind == "a":
                if half == 0:
                    stt["p0"] = p_t
                else:
                    on = nxt(onp, "on")
                    p0 = stt["p0"]
                    for sub in range(2):
                        mm(on[sub].v, p0.v[:, sub * 128:(sub + 1) * 128], v_t.v[:, st - 1, :], True, False,
                           r=[p0, v_t], w=[on[sub]])
                        mm(on[sub].v, p_t.v[:, sub * 128:(sub + 1) * 128], v_t.v[:, st, :], False, True,
                           r=[p_t, v_t], w=[on[sub]])
                    oa = stt["oa"]
                    for sub in range(2):
                        msk = maskg.v[:, 2 * m + sub, h, n:n + 1]
                        if n == 0:
                            vec("dve", "tensor_scalar", r=[on[sub], maskg], w=[oa], out=oa.v[:, sub, :], in0=on[sub].v,
                                scalar1=msk, scalar2=None, op0=ALU.mult)
                        else:
                            vec("dve", "scalar_tensor_tensor", r=[on[sub], maskg, oa], w=[oa], out=oa.v[:, sub, :],
                                in0=on[sub].v, scalar=msk, in1=oa.v[:, sub, :], op0=ALU.mult, op1=ALU.add)
                if last:
                    oa = stt["oa"]
                    rc_ = rcp[oc_[0] % 2]; oh = ohs[oc_[0] % 2]
                    vec("dve", "reciprocal", r=[oa], w=[rc_], out=rc_.v, in_=oa.v[:, :, 128])
                    for sub in range(2):
                        vec("dve", "tensor_scalar", r=[oa, rc_], w=[oh], out=oh.v[:, sub, :], in0=oa.v[:, sub, 0:128],
                            scalar1=rc_.v[:, sub:sub + 1], scalar2=None, op0=ALU.mult)
                    finish(idx, kind, h, m, oh)
                    oc_[0] += 1
            else:
                fo = stt["fo"]
                for sub in range(2):
                    mm(fo[sub].v, p_t.v[:, sub * 128:(sub + 1) * 128], v_t.v[:, st, :], st == 0, last,
                       r=[p_t, v_t], w=[fo[sub]])
                if last:
                    rc_ = rcp[oc_[0] % 2]; oh = ohs[oc_[0] % 2]
                    for sub in range(2):
                        vec("dve", "reciprocal", r=[fo[sub]], w=[rc_], out=rc_.v[:, sub:sub + 1], in_=fo[sub].v[:, 128:129])
                        vec("dve", "tensor_scalar", r=[fo[sub], rc_], w=[oh], out=oh.v[:, sub, :], in0=fo[sub].v[:, 0:128],
                            scalar1=rc_.v[:, sub:sub + 1], scalar2=None, op0=ALU.mult)
                    finish(idx, kind, h, m, oh)
                    oc_[0] += 1

        LOOK = 2
        pend = []

        def retire():
            it0, p0_ = pend.pop(0)
            pv(it0, p0_)
            idx0, _, _, m0_, st0 = it0
            if m0_ == NBO - 1 and st0 == 2 * (4 * m0_ + 4) - 1 and idx0 + 2 < len(heads):
                load_head(idx0 + 2)

        for it in items:
            if it[4] == 0:
                group_begin(it[0], it[1], it[2], it[3])
            p_t = qk(it)
            pend.append((it, p_t))
            if len(pend) > LOOK:
                retire()
        while pend:
            retire()
        P.barrier()
        A.off = mq
        if stop == "B":
            P.emit(block, sems)
            return nc

        hres = A.alloc([128, 4, 2048], F32)
        aT = A.alloc([128, 16, 512], BF16)
        ot = A.alloc([128, 16, 512], BF16)
        mT = A.alloc([128, 16, 512], BF16)
        wsl = A.ring(3, [128, 8192], BF16)
        sar = A.ring(4, [128, 512], F32)
        sfr = A.ring(4, [128, 512], F32)
        rlr = A.ring(2, [128, 512], F32)
        qcT = A.alloc([128, 4, 512], BF16)
        ptc = A.ring(2, [128, 512], BF16)
        octm = A.alloc([128, 4, 512], BF16)
        ocT = A.alloc([128, 4, 512], BF16)
        memt = []
        for j in range(2):
            tt = T(mT.v[:, j * 8:(j + 1) * 8, :].rearrange("p a b -> p (a b)").bitcast(F32)); tt.b = mT.b
            memt.append(tt)
        atmC = list(atm)
        for src_t in (octm, ocT):
            t_ = T(src_t.v.rearrange("p a b -> p (a b)")); t_.b = src_t.b
            atmC.append(t_)

        def norm_tile(g):
            outs = []
            for j in range(4):
                hj = T(hres.v[:, j, :]); hj.b = hres.b
                outs.append(norm_a(hj, g, atmC))
            for j in range(4):
                norm_tr(outs[j], aT, j * 128)

        wc = [0]
        if os.environ.get('K_STATS'):
            print('ARENA phaseC off', A.off, 'of', A.size, 'pers_mark', pers_mark, flush=True)
        gc = [1]

        PRE = 2
        plan = []
        plan.append(([16, 512], [lambda t: load_w(t, w_ck, 0, 16, 0, 512)]))
        plan.append(([16, 512], [lambda t: load_w(t, w_cv, 0, 16, 0, 512)]))
        for _tc in range(NO // 512):
            for fg in range(4):
                plan.append(([16, 512], [lambda t, fg=fg: load_w(t, w_in, 0, 16, C_GA + fg * 512, 512)]))
                plan.append(([16, 512], [lambda t, fg=fg: load_w(t, w_in, 0, 16, C_GF + fg * 512, 512)]))
                plan.append(([16, 512], [lambda t, fg=fg: load_w(t, w_bm, 0, 8, fg * 512, 512, k0=0),
                                         lambda t, fg=fg: load_w(t, w_bf, 0, 8, fg * 512, 512, k0=8)]))
            for cg in range(4):
                plan.append(([16, 512], [lambda t, cg=cg: load_w(t, w_mix, 0, 16, cg * 512, 512)]))
            plan.append(([16, 512], [lambda t: load_w(t, w_cq, 0, 16, 0, 512)]))
            plan.append(([4, 2048], [lambda t: load_w(t, w_co, 0, 4, 0, 2048)]))
            for fg in range(NFG):
                for q4 in range(4):
                    plan.append(([16, 512], [lambda t, fg=fg, q4=q4: load_w(t, w_ff1, 0, 16, fg * 2048 + q4 * 512, 512)]))
                for cg in range(4):
                    plan.append(([16, 512], [lambda t, fg=fg, cg=cg: load_w(t, w_ff2, fg * 2048, 16, cg * 512, 512)]))
        wst = {"issued": 0, "taken": 0, "tiles": {}}

        def wslot(shape3):
            i = wst["taken"]
            while wst["issued"] < min(len(plan), i + 1 + PRE):
                k = wst["issued"]
                shp, loaders = plan[k]
                w = wsl[k % 3]
                t = T(w.v[:, 0:shp[0] * shp[1]].rearrange("p (a b) -> p a b", a=shp[0])); t.b = w.b
                for fn in loaders:
                    fn(t)
                wst["tiles"][k] = t
                wst["issued"] += 1
            assert plan[i][0] == list(shape3), (i, plan[i][0], shape3)
            wst["taken"] += 1
            return wst["tiles"].pop(i)

        def load_g(gd):
            g = gbr[0]
            dma("sp", g.v, gd[:, :], w=[g])
            return g

        def add_res(i, sub, cg, pm):
            vec("dve", "tensor_tensor", r=[pm, hres], w=[hres], out=hres.v[:, sub, cg * 512:(cg + 1) * 512], in0=pm.v,
                in1=hres.v[:, sub, cg * 512:(cg + 1) * 512], op=ALU.add)

        gm = load_g(gmem_d)
        vec("pool", "memset", w=[Vc], ap=Vc.v, constant=1.0)
        for j in range(2):
            dma("sp", memt[j].v, memb[j * 128:(j + 1) * 128, :], w=[memt[j]])
            norm_T(memt[j], gm, aT, j * 128)
        wk = wslot([16, 512])
        for hc in range(4):
            pm = nxt(mmp, "mm")
            for dc in range(16):
                mm(pm.v[:, 0:256], wk.v[:, dc, hc * 128:(hc + 1) * 128], aT.v[:, dc, 0:256], dc == 0, dc == 15, r=[wk, aT], w=[pm])
            copy_any(hc, KcT.v[:, hc, :], pm.v[:, 0:256], r=[pm], w=[KcT])
        wv = wslot([16, 512])
        for j in range(2):
            pm = nxt(mmp, "mm")
            for dc in range(16):
                mm(pm.v, aT.v[:, dc, j * 128:(j + 1) * 128], wv.v[:, dc, :], dc == 0, dc == 15, r=[wv, aT], w=[pm])
            copy_any(j, Vc.v[:, j, :, 0:128], pm.v.rearrange("p (a b) -> p a b", a=4), r=[pm], w=[Vc])

        out_dmas = []
        for tc_ in range(NO // 512):
            t0 = tc_ * 512
            g1 = load_g(gmix_d)
            for j in range(4):
                hj = T(hres.v[:, j, :]); hj.b = hres.b
                dma("sp", hres.v[:, j, :], xo[t0 + j * 128:t0 + (j + 1) * 128, :], w=[hres])
            norm_tile(g1)
            dma("sp", ot.v, OTv[:, :, t0:t0 + 512], w=[ot])
            for fg in range(4):
                wga = wslot([16, 512])
                for f4 in range(4):
                    cs = slice(f4 * 128, (f4 + 1) * 128)
                    pga = nxt(mmp, "mm")
                    for dc in range(16):
                        mm(pga.v, wga.v[:, dc, cs], aT.v[:, dc, :], dc == 0, dc == 15, r=[wga, aT], w=[pga])
                    act(sar[f4].v, pga.v, AF.Sigmoid, r=[pga], w=[sar[f4]])
                wgf = wslot([16, 512])
                for f4 in range(4):
                    cs = slice(f4 * 128, (f4 + 1) * 128)
                    pgf = nxt(mmp, "mm")
                    for dc in range(16):
                        mm(pgf.v, wgf.v[:, dc, cs], aT.v[:, dc, :], dc == 0, dc == 15, r=[wgf, aT], w=[pgf])
                    act(sfr[f4].v, pgf.v, AF.Sigmoid, r=[pgf], w=[sfr[f4]])
                wb = wslot([16, 512])
                for f4 in range(4):
                    fc = fg * 4 + f4
                    cs = slice(f4 * 128, (f4 + 1) * 128)
                    pua = nxt(mmp, "mm")
                    for h in range(8):
                        mm(pua.v, wb.v[:, h, cs], ot.v[:, h, :], h == 0, h == 7, r=[wb, ot], w=[pua])
                    vec("dve", "tensor_tensor", r=[pua, sar[f4]], w=[sar[f4]], out=sar[f4].v, in0=pua.v, in1=sar[f4].v, op=ALU.mult)
                    puf = nxt(mmp, "mm")
                    for h in range(8):
                        mm(puf.v, wb.v[:, 8 + h, cs], ot.v[:, 8 + h, :], h == 0, h == 7, r=[wb, ot], w=[puf])
                    vec("dve", "tensor_tensor", r=[puf, sfr[f4]], w=[sfr[f4]], out=sfr[f4].v, in0=puf.v, in1=sfr[f4].v, op=ALU.mult)
                    vec("pool", "tensor_tensor", r=[sar[f4], sfr[f4]], w=[mT], out=mT.v[:, fc, :], in0=sar[f4].v, in1=sfr[f4].v, op=ALU.add)
            for cg in range(4):
                wm = wslot([16, 512])
                for sub in range(4):
                    pm = nxt(mmp, "mm")
                    for fc in range(16):
                        mm(pm.v, mT.v[:, fc, sub * 128:(sub + 1) * 128], wm.v[:, fc, :], fc == 0, fc == 15, r=[mT, wm], w=[pm])
                    add_res(0, sub, cg, pm)
            g2 = load_g(gcross_d)
            norm_tile(g2)
            wq = wslot([16, 512])
            for hc in range(4):
                pm = nxt(mmp, "mm")
                for dc in range(16):
                    mm(pm.v, wq.v[:, dc, hc * 128:(hc + 1) * 128], aT.v[:, dc, :], dc == 0, dc == 15, r=[wq, aT], w=[pm])
                copy_any(hc, qcT.v[:, hc, :], pm.v, r=[pm], w=[qcT])
            wo = wslot([4, 2048])
            for hc in range(4):
                pts = []
                for st in range(2):
                    ps = nxt(mmp, "mm")
                    mm(ps.v, KcT.v[:, hc, st * 128:(st + 1) * 128], qcT.v[:, hc, :], True, True, r=[KcT, qcT], w=[ps])
                    p_t = ptc[st]
                    act(p_t.v, ps.v, AF.Exp, r=[ps], w=[p_t], scale=SCALE)
                    pts.append(p_t)
                for sub in range(4):
                    po = misc[sub % 2]
                    for st in range(2):
                        mm(po.v[:, 0:129], pts[st].v[:, sub * 128:(sub + 1) * 128], Vc.v[:, st, hc, :], st == 0, st == 1,
                           r=[pts[st], Vc], w=[po])
                    rc_ = rstdr[rc["ss"] % 4]; rc["ss"] += 1
                    vec("dve", "reciprocal", r=[po], w=[rc_], out=rc_.v, in_=po.v[:, 128:129])
                    vec("dve", "tensor_scalar", r=[po, rc_], w=[octm], out=octm.v[:, sub, hc * 128:(hc + 1) * 128], in0=po.v[:, 0:128],
                        scalar1=rc_.v[:, 0:1], scalar2=None, op0=ALU.mult)
            for sub in range(4):
                pt = nxt(trp, "tr")
                for hc in range(4):
                    tr(pt.v[:, hc * 128:(hc + 1) * 128], octm.v[:, sub, hc * 128:(hc + 1) * 128], identb.v, r=[octm, identb], w=[pt])
                copy_any(sub, ocT.v[:, :, sub * 128:(sub + 1) * 128], pt.v[:, 0:512].rearrange("p (a b) -> p a b", a=4),
                         r=[pt], w=[ocT])
            for cg in range(4):
                for sub in range(4):
                    pm = nxt(mmp, "mm")
                    for hc in range(4):
                        mm(pm.v, ocT.v[:, hc, sub * 128:(sub + 1) * 128], wo.v[:, hc, cg * 512:(cg + 1) * 512], hc == 0, hc == 3,
                           r=[ocT, wo], w=[pm])
                    add_res(0, sub, cg, pm)
            g3 = load_g(gmlp_d)
            norm_tile(g3)
            for fg in range(NFG):
                for q4 in range(4):
                    w1 = wslot([16, 512])
                    for f4 in range(4):
                        fc = q4 * 4 + f4
                        pm = nxt(mmp, "mm")
                        for dc in range(16):
                            mm(pm.v, w1.v[:, dc, f4 * 128:(f4 + 1) * 128], aT.v[:, dc, :], dc == 0, dc == 15, r=[w1, aT], w=[pm])
                        rl = rlr[fc % 2]
                        act(rl.v, pm.v, AF.Relu, r=[pm], w=[rl])
                        vec("pool", "tensor_tensor", r=[rl], w=[ot], out=ot.v[:, fc, :], in0=rl.v, in1=rl.v, op=ALU.mult)
                for cg in range(4):
                    w2 = wslot([16, 512])
                    for sub in range(4):
                        pm = nxt(mmp, "mm")
                        for fc in range(16):
                            mm(pm.v, ot.v[:, fc, sub * 128:(sub + 1) * 128], w2.v[:, fc, :], fc == 0, fc == 15, r=[ot, w2], w=[pm])
                        add_res(0, sub, cg, pm)
            g4 = load_g(gfin_d)
            for j in range(4):
                ss = ssr[rc["ss"] % 4]; rstd = rstdr[rc["ss"] % 4]; rc["ss"] += 1
                vec("pool", "memset", w=[ss], ap=ss.v, constant=0.0)
                act(sqj.v, hres.v[:, j, :], AF.Square, r=[hres], w=[sqj, ss], accum_out=ss.v)
                act(rstd.v, ss.v, AF.Sqrt, r=[ss], w=[rstd], bias=EPS, scale=1.0 / D)
                vec("dve", "reciprocal", r=[rstd], w=[rstd], out=rstd.v, in_=rstd.v)
                vec("dve", "scalar_tensor_tensor", r=[hres, rstd, g4], w=[hres], out=hres.v[:, j, :], in0=hres.v[:, j, :],
                    scalar=rstd.v[:, 0:1], in1=g4.v, op0=ALU.mult, op1=ALU.mult)
                out_dmas.append(dma("sp", out_d[t0 + j * 128:t0 + (j + 1) * 128, :], hres.v[:, j, :], r=[hres]))
        P.add("sp", lambda: nc.sync.nop(), extra=out_dmas)
        P.barrier()
        P.emit(block, sems)
    return nc


def _t5_bucket_table(maxd):
    n = np.arange(maxd + 1)
    nf = np.maximum(n, 1).astype(np.float32)
    ratio = (np.log((nf / np.float32(16)).astype(np.float32)).astype(np.float32)
             / np.float32(math.log(1024 / 16))).astype(np.float32)
    large = 16 + (ratio * np.float32(16)).astype(np.float32).astype(np.int32)
    large = np.minimum(large, 31)
    return np.where(n < 16, n, large)


def _core_tables(r, NB, NBO):
    bt = _t5_bucket_table(8 * 256 + 512)
    oh = np.zeros((33, 8, 512), np.float32)
    u = np.arange(511)
    for ni in range(8):
        npr = ni - 4
        dist = (r - npr) * 256 + (u - 255)
        b = np.where(dist < 0, 32, bt[np.clip(dist, 0, len(bt) - 1)])
        oh[b, ni, u] = 1.0
    fm = np.zeros((128, 4, 2, 256), np.float32)
    s = np.arange(128)[:, None]
    q = np.arange(256)[None, :]
    for npr in range(4):
        for half in range(2):
            if npr < r:
                continue
            if npr > r:
                fm[:, npr, half, :] = NEG
            else:
                fm[:, npr, half, :] = np.where(half * 128 + s > q, NEG, 0.0)
    sel = np.zeros((128, 8, 128), np.float32)
    sel[0, 2 * r + 1, :] = 1.0
    elneg = np.zeros((NBO, NB), np.float32)
    el01 = np.zeros((NBO, NB), np.float32)
    own = np.zeros((NBO, NB), np.float32)
    for m in range(NBO):
        o = 4 * m + r
        el01[m, :o] = 1.0
        elneg[m, o:] = -1e30
        own[m, o] = 1.0
    bc = lambda a: np.ascontiguousarray(np.broadcast_to(a.reshape(1, -1), (128, a.size)))
    return dict(oh33=oh.reshape(33, 4096), fmask=fm.reshape(128, 2048), sel=sel.reshape(128, 1024),
                eligneg=bc(elneg), elig01=bc(el01), own01=bc(own))


def _prep(inputs, S):
    NB = S // 256
    NBO = NB // 4
    f = lambda a: np.ascontiguousarray(np.asarray(a, dtype=np.float32))
    x = f(inputs["x"]); mem = f(inputs["mem"])
    bc = lambda g: np.ascontiguousarray(np.broadcast_to(f(g).reshape(1, D), (128, D)))
    rel = f(inputs["rel_bias"])
    shared = dict(
        w_in=f(inputs["w_in"][0]), w_bm=f(inputs["w_branch_moba"][0]), w_bf=f(inputs["w_branch_fox"][0]),
        w_mix=f(inputs["w_mix_out"][0]), w_cq=f(inputs["w_cq"][0]), w_ck=f(inputs["w_ck"][0]), w_cv=f(inputs["w_cv"][0]),
        w_co=f(inputs["w_co"][0]), w_ff1=f(inputs["w_ff1"][0]), w_ff2=f(inputs["w_ff2"][0]),
        g_mix_b=bc(inputs["g_mix"][0]), g_cross_b=bc(inputs["g_cross"][0]), g_mem_b=bc(inputs["g_mem"][0]),
        g_mlp_b=bc(inputs["g_mlp"][0]), g_final_b=bc(inputs["g_final"]),
        b_forget_c=f(inputs["b_forget"][0]).reshape(8, 1), rbT=np.ascontiguousarray(rel.T),
        rb31b=np.ascontiguousarray(np.broadcast_to(rel[:, 31].reshape(1, 8), (128, 8))),
        ident=np.eye(128, dtype=np.float32),
    )
    tabs = [_core_tables(r, NB, NBO) for r in range(4)]
    in_maps = []
    rows = []
    for c in range(8):
        b, r = c // 4, c % 4
        xbb = x[b]
        xbr = np.ascontiguousarray(xbb.reshape(S // 128, 128, D)[:, ::-1, :].reshape(S, D))
        idx = np.concatenate([np.arange((4 * m + r) * 256, (4 * m + r + 1) * 256) for m in range(NBO)])
        rows.append((b, idx))
        mp = dict(shared)
        mp.update(tabs[r])
        mp.update(xb=xbb, xbr=xbr, xo=np.ascontiguousarray(xbb[idx]), memb=mem[b])
        in_maps.append(mp)
    return in_maps, rows


_NC_CACHE = {}


def _run(inputs, S, DFF, dbg=False, stop=None):
    key = (S, DFF, dbg)
    if key not in _NC_CACHE:
        _NC_CACHE[key] = build(S, DFF, dbg, stop)
    nc = _NC_CACHE[key]
    in_maps, rows = _prep(inputs, S)
    ncr = int(os.environ.get("K_NCORES", "8"))
    res = run_bass_kernel_spmd(nc, in_maps[:ncr], core_ids=list(range(ncr)))
    out = np.zeros((2, S, D), np.float32)
    for c in range(ncr):
        b, idx = rows[c]
        out[b, idx] = res.results[c]["out"]
    if dbg:
        return out, res
    return out


def kernel(**inputs):
    return _run(inputs, 8192, 8192)
```

```python
import math
import os
from contextlib import ExitStack
import numpy as np
import concourse.bass as bass
import concourse.mybir as mybir
from concourse.bass_utils import run_bass_kernel_spmd

F32 = mybir.dt.float32
BF16 = mybir.dt.bfloat16
U8 = mybir.dt.uint8
AF = mybir.ActivationFunctionType
ALU = mybir.AluOpType
AX = mybir.AxisListType

ENGS = ("pe", "act", "dve", "pool", "sp")

D = 2048
DC = 16
NH = 8
C_QA, C_KA, C_VA, C_QF, C_KF, C_VF, C_FL, C_GA, C_GF = 0, 1024, 2048, 3072, 4096, 5120, 6144, 6152, 8200
IN_W = 10248
SCALE = 128 ** -0.5
EPS = 1e-6
NEG = -30000.0


class Buf:
    __slots__ = ("writers", "readers")

    def __init__(self):
        self.writers = []
        self.readers = {}


class Ins:
    __slots__ = ("eng", "fn", "deps", "is_dma", "marked", "sem", "val")

    def __init__(self, eng, fn, is_dma):
        self.eng = eng
        self.fn = fn
        self.deps = []
        self.is_dma = is_dma
        self.marked = False
        self.sem = None
        self.val = 0


class Prog:
    N_DMA_SEMS = 14

    def __init__(self, nc):
        self.nc = nc
        self.q = {e: [] for e in ENGS}
        self.last = {e: None for e in ENGS}
        self.dmas = []

    def add(self, eng, fn, r=(), w=(), is_dma=False, extra=()):
        ins = Ins(eng, fn, is_dma)
        deps = list(extra)
        for b in r:
            deps.extend(b.writers)
        for b in w:
            deps.extend(b.writers)
            deps.extend(b.readers.values())
        seen = set()
        for d in deps:
            if d is ins or id(d) in seen:
                continue
            seen.add(id(d))
            if (not d.is_dma) and (not is_dma) and d.eng == eng and eng == "pe":
                continue
            ins.deps.append(d)
            d.marked = True
        for b in r:
            key = ("dma", id(ins)) if is_dma else eng
            b.readers[key] = ins
        for b in w:
            b.writers = [ins]
            b.readers = {}
        self.q[eng].append(ins)
        if is_dma:
            self.dmas.append(ins)
        else:
            self.last[eng] = ins
        return ins

    def barrier(self):
        nc = self.nc
        eo = {"pe": nc.tensor, "act": nc.scalar, "dve": nc.vector, "pool": nc.gpsimd, "sp": nc.sync}
        lasts = [v for v in self.last.values() if v is not None]
        dm = list(self.dmas)
        self.dmas = []
        for e in ENGS:
            self.add(e, (lambda o=eo[e]: o.nop()), extra=[x for x in lasts if x.eng != e] + dm)

    def emit(self, block, sems):
        for e in ENGS:
            cnt = 0
            dma_cnt = [0] * self.N_DMA_SEMS
            last_on = [None] * self.N_DMA_SEMS
            rr = 0
            for ins in self.q[e]:
                if ins.is_dma:
                    k = rr % self.N_DMA_SEMS
                    rr += 1
                    dma_cnt[k] += 16
                    ins.sem = sems["dma_" + e][k]
                    ins.val = dma_cnt[k]
                    if last_on[k] is not None:
                        ins.deps.append(last_on[k])
                    last_on[k] = ins
                elif ins.marked:
                    cnt += 1
                    ins.sem = sems[e]
                    ins.val = cnt
            if os.environ.get("K_STATS"):
                print("STATS", e, "n_ins", len(self.q[e]), "marked", cnt, "dma_max", max(dma_cnt), flush=True)
        nc = self.nc
        engobj = {"pe": nc.tensor, "act": nc.scalar, "dve": nc.vector, "pool": nc.gpsimd, "sp": nc.sync}

        def run(e):
            def body(_eng):
                eo = engobj[e]
                waited = {}
                for ins in self.q[e]:
                    need = {}
                    for d in ins.deps:
                        s = d.sem
                        if d.val > need.get(id(s), (None, 0))[1]:
                            need[id(s)] = (s, d.val)
                    for sid, (s, v) in need.items():
                        if waited.get(sid, 0) < v:
                            eo.wait_ge(s, v)
                            waited[sid] = v
                    bi = ins.fn()
                    if ins.is_dma:
                        bi.then_inc(ins.sem, 16)
                    elif ins.marked:
                        bi.then_inc(ins.sem, 1)
            return body

        block.tensor(run("pe"))
        block.scalar(run("act"))
        block.vector(run("dve"))
        block.gpsimd(run("pool"))
        block.sync(run("sp"))


class T:
    __slots__ = ("v", "b")

    def __init__(self, v):
        self.v = v
        self.b = Buf()


class Arena:
    def __init__(self, t, size):
        self.t = t
        self.size = size
        self.off = 0

    def alloc(self, shape, dt):
        esz = 4 if dt == F32 else 2
        n = int(np.prod(shape[1:])) * esz
        assert self.off + n <= self.size, ("sbuf arena overflow", self.off, n, self.size)
        v = self.t[0:shape[0], self.off:self.off + n].bitcast(dt)
        self.off += (n + 63) // 64 * 64
        if len(shape) == 3:
            v = v.rearrange("p (a b) -> p a b", a=shape[1])
        elif len(shape) == 4:
            v = v.rearrange("p (a b c) -> p a b c", a=shape[1], b=shape[2])
        return T(v)

    def ring(self, n, shape, dt):
        return [self.alloc(shape, dt) for _ in range(n)]


ARENA_BYTES = 212800


def build(S, DFF, dbg=False, stop=None):
    NB = S // 256
    NBO = NB // 4
    NO = NBO * 256
    NT = S // 128
    NFG = DFF // 2048
    nc = bass.Bass("TRN2", target_bir_lowering=False)

    def din(name, shape):
        return nc.dram_tensor(name, list(shape), F32, kind="ExternalInput").ap()

    kscr = "ExternalOutput" if dbg else "Internal"

    def dscr(name, shape, dt=BF16):
        return nc.dram_tensor(name, list(shape), dt, kind=kscr)

    xb = din("xb", [S, D]); xbr = din("xbr", [S, D]); xo = din("xo", [NO, D]); memb = din("memb", [256, D])
    w_in = din("w_in", [D, IN_W]); w_bm = din("w_bm", [1024, D]); w_bf = din("w_bf", [1024, D])
    w_mix = din("w_mix", [D, D]); w_cq = din("w_cq", [D, 512]); w_ck = din("w_ck", [D, 512])
    w_cv = din("w_cv", [D, 512]); w_co = din("w_co", [512, D]); w_ff1 = din("w_ff1", [D, DFF])
    w_ff2 = din("w_ff2", [DFF, D])
    gmix_d = din("g_mix_b", [128, D]); gcross_d = din("g_cross_b", [128, D]); gmem_d = din("g_mem_b", [128, D])
    gmlp_d = din("g_mlp_b", [128, D]); gfin_d = din("g_final_b", [128, D])
    bf_d = din("b_forget_c", [8, 1]); rbT_d = din("rbT", [32, 8]); rb31_d = din("rb31b", [128, 8])
    oh_d = din("oh33", [33, 4096]); fmask_d = din("fmask", [128, 2048]); sel_d = din("sel", [128, 1024])
    elneg_d = din("eligneg", [128, NBO * NB]); el01_d = din("elig01", [128, NBO * NB])
    own01_d = din("own01", [128, NBO * NB]); ident_d = din("ident", [128, 128])
    out_d = nc.dram_tensor("out", [NO, D], F32, kind="ExternalOutput").ap()

    KTf = dscr("KTf", [8, 128, S]); KTa = dscr("KTa", [8, 128, S])
    Vf = dscr("Vf", [8, 128, NT, 129]); Va = dscr("Va", [8, 128, NT, 129])
    QTs = dscr("QTs", [16, 128, NO])
    OTs = dscr("OTs", [16, 128, NO])
    Fs = dscr("Fs", [8, 4096])

    es = ExitStack()
    with es:
        arena_t = es.enter_context(nc.sbuf_tensor("arena", [128, ARENA_BYTES], U8))
        A = Arena(arena_t, ARENA_BYTES)
        pbank = [es.enter_context(nc.psum_tensor(f"pb{i}", [128, 512], F32)) for i in range(8)]
        sems = {}
        for e in ENGS:
            sems[e] = es.enter_context(nc.semaphore("s_" + e))
        for e in ("sp", "pool", "act"):
            sems["dma_" + e] = [es.enter_context(nc.semaphore(f"d_{e}{k}")) for k in range(Prog.N_DMA_SEMS)]
        block = es.enter_context(nc.Block())
        P = Prog(nc)

        def bufs(ts):
            return [t.b for t in ts]

        def dma(eng, out, in_, r=(), w=()):
            eo = {"sp": nc.sync, "pool": nc.gpsimd, "act": nc.scalar}[eng]
            return P.add(eng, lambda: eo.dma_start(out=out, in_=in_), r=bufs(r), w=bufs(w), is_dma=True)

        def mm(out, lhsT, rhs, start, stop, r=(), w=()):
            return P.add("pe", lambda: nc.tensor.matmul(out, lhsT=lhsT, rhs=rhs, start=start, stop=stop),
                         r=bufs(r), w=bufs(w))

        def tr(out, in_, ident, r=(), w=()):
            return P.add("pe", lambda: nc.tensor.transpose(out=out, in_=in_, identity=ident), r=bufs(r), w=bufs(w))

        def act(out, in_, func, r=(), w=(), **kw):
            return P.add("act", lambda: nc.scalar.activation(out=out, in_=in_, func=func, **kw), r=bufs(r), w=bufs(w))

        def vec(eng, name, r=(), w=(), **kw):
            eo = {"dve": nc.vector, "pool": nc.gpsimd}[eng]
            return P.add(eng, lambda: getattr(eo, name)(**kw), r=bufs(r), w=bufs(w))

        def copy_any(i, out, in_, r=(), w=()):
            if i % 2 == 0:
                return act(out, in_, AF.Copy, r=r, w=w)
            return vec("dve", "tensor_copy", r=r, w=w, out=out, in_=in_)

        def dbg_dump(name, t, shape):
            if not dbg:
                return
            dd = nc.dram_tensor("dbg_" + name, list(shape), F32, kind="ExternalOutput").ap()
            if len(shape) == 3:
                dma("sp", dd[:, :, :], t.v, r=[t])
            elif len(shape) == 4:
                dma("sp", dd[:, :, :, :], t.v, r=[t])
            else:
                dma("sp", dd[:, :], t.v, r=[t])

        class PS:
            pass
        mmp = [T(pbank[i][:, :]) for i in range(4)]
        trp = [T(pbank[4 + i][:, :].bitcast(BF16)) for i in range(2)]
        misc = [T(pbank[6 + i][:, :]) for i in range(2)]
        stp = [T(pbank[i][:, 0:256]) for i in range(2)]
        for i in range(2):
            t_ = T(pbank[4 + i][:, 0:256]); t_.b = trp[i].b
            stp.append(t_)
        onp = []
        for bk in range(2):
            pair = [T(pbank[2 + bk][:, sub * 256:sub * 256 + 129]) for sub in range(2)]
            pair[1].b = pair[0].b
            onp.append(pair)
        fop = [T(pbank[6][:, 0:129]), T(pbank[7][:, 0:129])]
        cnt = {"mm": 0, "tr": 0, "misc": 0, "st": 0, "on": 0, "ev": 0}

        def nxt(pool, key):
            i = cnt[key]
            cnt[key] += 1
            return pool[i % len(pool)]

        identb = A.alloc([128, 128], BF16); ident32 = A.alloc([128, 128], F32)
        negc = A.alloc([128, NT, 8], F32)
        kbarT = A.alloc([128, 8, NB], F32)
        elneg = A.alloc([128, NBO * NB], F32); el01 = A.alloc([128, NBO * NB], F32); own01 = A.alloc([128, NBO * NB], F32)
        rb31 = A.alloc([128, 8], F32)
        fmask = A.alloc([128, 8, 256], BF16)
        KcT = A.alloc([128, 4, 256], BF16); Vc = A.alloc([128, 2, 4, 129], BF16)
        znb = A.alloc([128, NBO, 8], F32)
        negb = A.alloc([8, 1], F32); ones8 = A.alloc([8, 512], F32)
        selt = A.alloc([128, 8, 128], F32)
        ssr = A.ring(4, [128, 1], F32); rstdr = A.ring(4, [128, 1], F32)
        sqj = A.alloc([128, 2048], BF16)
        gbr = A.ring(1, [128, 2048], F32)
        atm = A.ring(2, [128, 2048], BF16)
        pers_mark = A.off
        rc = {"ss": 0, "atm": 0, "xin": 0}

        dma("sp", ident32.v, ident_d[:, :], w=[ident32])
        vec("dve", "tensor_copy", r=[ident32], w=[identb], out=identb.v, in_=ident32.v)
        dma("sp", elneg.v, elneg_d[:, :], w=[elneg]); dma("sp", el01.v, el01_d[:, :], w=[el01])
        dma("sp", own01.v, own01_d[:, :], w=[own01]); dma("sp", rb31.v, rb31_d[:, :], w=[rb31])
        dma("pool", fmask.v, fmask_d.rearrange("p (a b) -> p a b", a=8), w=[fmask])
        dma("sp", selt.v, sel_d.rearrange("p (a b) -> p a b", a=8), w=[selt])
        dma("sp", negb.v, bf_d[:, :], w=[negb])
        vec("dve", "tensor_scalar", r=[negb], w=[negb], out=negb.v, in0=negb.v, scalar1=-1.0, scalar2=None, op0=ALU.mult)
        vec("pool", "memset", w=[ones8], ap=ones8.v, constant=1.0)

        def norm_a(xt, gb, ring=None):
            ring = ring or atm
            ss = ssr[rc["ss"] % 4]; rstd = rstdr[rc["ss"] % 4]; rc["ss"] += 1
            a = ring[rc["atm"] % len(ring)]; rc["atm"] += 1
            vec("pool", "memset", w=[ss], ap=ss.v, constant=0.0)
            act(sqj.v, xt.v, AF.Square, r=[xt], w=[sqj, ss], accum_out=ss.v)
            act(rstd.v, ss.v, AF.Sqrt, r=[ss], w=[rstd], bias=EPS, scale=1.0 / D)
            vec("dve", "reciprocal", r=[rstd], w=[rstd], out=rstd.v, in_=rstd.v)
            vec("dve", "scalar_tensor_tensor", r=[xt, rstd, gb], w=[a], out=a.v, in0=xt.v, scalar=rstd.v[:, 0:1],
                in1=gb.v, op0=ALU.mult, op1=ALU.mult)
            return a

        def norm_tr(a, dst, c0):
            for g8 in range(2):
                pt = nxt(trp, "tr")
                for k in range(8):
                    dc = g8 * 8 + k
                    tr(pt.v[:, k * 128:(k + 1) * 128], a.v[:, dc * 128:(dc + 1) * 128], identb.v, r=[a, identb], w=[pt])
                copy_any(cnt["ev"], dst.v[:, g8 * 8:(g8 + 1) * 8, c0:c0 + 128],
                         pt.v.rearrange("p (a b) -> p a b", a=8), r=[pt], w=[dst])
                cnt["ev"] += 1

        def norm_T(xt, gb, dst, c0):
            norm_tr(norm_a(xt, gb), dst, c0)

        def load_w(dst, wd, r0, kc, c0, ncols, k0=0, col_off=0):
            src = wd[r0:r0 + kc * 128, c0:c0 + ncols].rearrange("(c p) n -> p c n", p=128)
            step = max(1, min(kc, 4096 // ncols))
            for k in range(0, kc, step):
                kk = min(step, kc - k)
                dma("pool", dst.v[:, k0 + k:k0 + k + kk, col_off:col_off + ncols], src[:, k:k + kk, :], w=[dst])

        m0 = A.off
        rbx = A.alloc([33, 8], F32); oht = A.alloc([33, 4096], F32); fst = A.alloc([8, 4096], BF16)
        vec("pool", "memset", w=[rbx], ap=rbx.v[32:33, :], constant=NEG)
        dma("sp", rbx.v[0:32, :], rbT_d[:, :], w=[rbx])
        dma("sp", oht.v, oh_d[:, :], w=[oht])
        for k in range(8):
            pm = nxt(misc, "misc")
            mm(pm.v[0:8, :], rbx.v, oht.v[:, k * 512:(k + 1) * 512], True, True, r=[rbx, oht], w=[pm])
            act(fst.v[:, k * 512:(k + 1) * 512], pm.v[0:8, :], AF.Copy, r=[pm], w=[fst], scale=1.0 / SCALE)
        Fs_t = T(None)
        dma("sp", Fs.ap()[:, :], fst.v, r=[fst], w=[Fs_t])
        P.barrier()
        A.off = m0
        if stop == "t5":
            P.emit(block, sems)
            return nc

        def phase_A(moba):
            m0 = A.off
            W = A.alloc([128, 16, 2048], BF16)
            Wf = A.alloc([128, 16, 8], BF16)
            xin = A.ring(3, [128, 2048], F32)
            aT = A.ring(2, [128, 16, 512], BF16)
            kst = A.ring(4, [128, 512], BF16)
            vst = A.ring(2, [128, 8, 129], BF16)
            et = A.alloc([8, 512], F32); lt = A.alloc([8, 512], F32); nct = A.ring(2, [8, 512], F32)
            gb = gbr[0]
            dma("sp", gb.v, gmix_d[:, :], w=[gb])
            load_w(W, w_in, 0, 16, C_KA if moba else C_KF, 1024)
            load_w(W, w_in, 0, 16, C_VA if moba else C_VF, 1024, col_off=1024)
            if not moba:
                load_w(Wf, w_in, 0, 16, C_FL, 8)
            for v in vst:
                vec("pool", "memset", w=[v], ap=v.v, constant=1.0)
            if moba:
                vec("pool", "memset", w=[kbarT], ap=kbarT.v, constant=0.0)
            xsrc = xbr if moba else xb
            KT = KTa if moba else KTf
            VS = Va if moba else Vf
            VSv = VS.ap().rearrange("h p t c -> p h t c")
            atmA = A.ring(5, [128, 2048], BF16)

            def do_norm_a(t):
                outs = []
                for j in range(4):
                    xt = xin[rc["xin"] % 3]; rc["xin"] += 1
                    r0 = t * 512 + j * 128
                    dma("sp", xt.v, xsrc[r0:r0 + 128, :], w=[xt])
                    outs.append(norm_a(xt, gb, atmA))
                return outs

            def do_norm_tr(t, outs):
                for j in range(4):
                    norm_tr(outs[j], aT[t % 2], j * 128)

            do_norm_tr(0, do_norm_a(0))
            for t in range(S // 512):
                a_t = aT[t % 2]
                nxt_a = do_norm_a(t + 1) if t + 1 < S // 512 else None
                for h in range(8):
                    pm = nxt(mmp, "mm")
                    for dc in range(16):
                        mm(pm.v, W.v[:, dc, h * 128:(h + 1) * 128], a_t.v[:, dc, :], dc == 0, dc == 15,
                           r=[W, a_t], w=[pm])
                    ks = kst[(t * 8 + h) % 4]
                    if moba:
                        for i2 in range(2):
                            act(ks.v[:, i2 * 256:(i2 + 1) * 256], pm.v[:, i2 * 256:(i2 + 1) * 256], AF.Copy, r=[pm], w=[ks, kbarT],
                                accum_out=kbarT.v[:, h, 2 * t + i2:2 * t + i2 + 1])
                    else:
                        copy_any(cnt["ev"], ks.v, pm.v, r=[pm], w=[ks]); cnt["ev"] += 1
                    dma("sp", KT.ap()[h, :, t * 512:(t + 1) * 512], ks.v, r=[ks])
                if nxt_a is not None:
                    do_norm_tr(t + 1, nxt_a)
                for j in range(4):
                    vs = vst[(t * 4 + j) % 2]
                    for g in range(2):
                        pm = nxt(mmp, "mm")
                        for dc in range(16):
                            mm(pm.v, a_t.v[:, dc, j * 128:(j + 1) * 128], W.v[:, dc, 1024 + g * 512:1024 + (g + 1) * 512],
                               dc == 0, dc == 15, r=[W, a_t], w=[pm])
                        copy_any(cnt["ev"], vs.v[:, g * 4:(g + 1) * 4, 0:128], pm.v.rearrange("p (a b) -> p a b", a=4),
                                 r=[pm], w=[vs]); cnt["ev"] += 1
                    dma("sp", VSv[:, :, t * 4 + j, :], vs.v, r=[vs])
                if not moba:
                    pm = nxt(misc, "misc")
                    for dc in range(16):
                        mm(pm.v[0:8, :], Wf.v[:, dc, :], a_t.v[:, dc, :], dc == 0, dc == 15, r=[Wf, a_t], w=[pm])
                    act(et.v, pm.v[0:8, :], AF.Exp, r=[pm, negb], w=[et], bias=negb.v[:, 0:1], scale=-1.0)
                    act(lt.v, et.v, AF.Ln, r=[et], w=[lt], bias=1.0)
                    ncur = nct[t % 2]; nprev = nct[(t + 1) % 2]
                    if t == 0:
                        vec("dve", "tensor_tensor_scan", r=[ones8, lt], w=[ncur], out=ncur.v, data0=ones8.v, data1=lt.v,
                            initial=0.0, op0=ALU.mult, op1=ALU.add)
                    else:
                        vec("dve", "tensor_tensor_scan", r=[ones8, lt, nprev], w=[ncur], out=ncur.v, data0=ones8.v,
                            data1=lt.v, initial=nprev.v[:, 511:512], op0=ALU.mult, op1=ALU.add)
                    pm2 = nxt(misc, "misc")
                    for j in range(4):
                        tr(pm2.v[:, j * 8:(j + 1) * 8], ncur.v[:, j * 128:(j + 1) * 128], ident32.v[0:8, 0:8],
                           r=[ncur, ident32], w=[pm2])
                    vec("dve", "tensor_copy", r=[pm2], w=[negc], out=negc.v[:, 4 * t:4 * t + 4, :],
                        in_=pm2.v[:, 0:32].rearrange("p (a b) -> p a b", a=4))
            P.barrier()
            A.off = m0

        phase_A(False)
        if stop == "A1":
            P.emit(block, sems)
            return nc
        phase_A(True)
        if stop == "A2":
            P.emit(block, sems)
            return nc

        mq = A.off
        maskg = A.alloc([128, NO // 128, 8, NB], F32)
        mq2 = A.off
        W = A.alloc([128, 16, 2048], BF16)
        xin = A.ring(3, [128, 2048], F32)
        aT = A.ring(2, [128, 16, 256], BF16)
        qst = A.ring(2, [128, 16, 256], BF16)
        q32 = A.ring(2, [128, 256], F32)
        gt = A.ring(2, [128, 8, NB], F32)
        top8 = A.ring(2, [128, 8], F32)
        atmQ = A.ring(4, [128, 2048], BF16)
        gb = gbr[0]
        dma("sp", gb.v, gmix_d[:, :], w=[gb])
        load_w(W, w_in, 0, 16, C_QA, 1024)
        load_w(W, w_in, 0, 16, C_QF, 1024, col_off=1024)
        QCUT = int(os.environ.get("Q_CUT", "99"))
        QGM = int(os.environ.get('Q_GM', '1'))
        for m in range(min(NBO, int(os.environ.get('Q_SLOTS', '99'))) if QCUT >= 2 else 0):
            a_t = aT[m % 2]; qs = qst[m % 2]

            def do_norm_q_a(mm_):
                outs = []
                for j in range(2):
                    xt = xin[rc["xin"] % 3]; rc["xin"] += 1
                    r0 = mm_ * 256 + j * 128
                    dma("sp", xt.v, xo[r0:r0 + 128, :], w=[xt])
                    outs.append(norm_a(xt, gb, atmQ))
                return outs

            def do_norm_q_tr(mm_, outs):
                for j in range(2):
                    norm_tr(outs[j], aT[mm_ % 2], j * 128)

            if m == 0:
                do_norm_q_tr(0, do_norm_q_a(0))
            nxt_q = do_norm_q_a(m + 1) if m + 1 < NBO else None
            pg = [nxt(misc, "misc"), nxt(misc, "misc")]
            for h in range(16 if QCUT >= 3 else 0):
                pm = nxt(mmp, "mm")
                for dc in range(16):
                    mm(pm.v[:, 0:256], W.v[:, dc, h * 128:(h + 1) * 128], a_t.v[:, dc, :], dc == 0, dc == 15,
                       r=[W, a_t], w=[pm])
                copy_any(cnt["ev"], qs.v[:, h, :], pm.v[:, 0:256], r=[pm], w=[qs]); cnt["ev"] += 1
                if h == 7 and nxt_q is not None:
                    do_norm_q_tr(m + 1, nxt_q)
                if h < 8 and QCUT >= 4:
                    qq = q32[h % 2]
                    QQ = int(os.environ.get('Q_QQ', '6'))
                    if QQ == 0:
                        vec("dve", "tensor_copy", r=[pm], w=[qq], out=qq.v, in_=pm.v[:, 0:256])
                    elif QQ == 1:
                        act(qq.v, pm.v[:, 0:256], AF.Copy, r=[pm], w=[qq])
                    elif QQ == 2:
                        vec("dve", "tensor_copy", r=[qs], w=[qq], out=qq.v, in_=qs.v[:, h, :])
                    elif QQ == 6:
                        vec("dve", "tensor_copy", r=[pm, qs], w=[qq], out=qq.v, in_=pm.v[:, 0:256])
                    elif QQ == 3:
                        vec("pool", "tensor_copy", r=[qs], w=[qq], out=qq.v, in_=qs.v[:, h, :])
                    for j in range(2 if QGM else 0):
                        rhs_ = kbarT.v[:, h, :] if QGM == 1 else ident32.v[:, 0:NB]
                        mm(pg[j].v[:, h * NB:(h + 1) * NB], qq.v[:, j * 128:(j + 1) * 128], rhs_, True, True,
                           r=[qq, kbarT, ident32], w=[pg[j]])
            if QCUT < 5:
                continue
            dma("sp", QTs.ap().rearrange("h p t -> p h t")[:, :, m * 256:(m + 1) * 256], qs.v, r=[qs])
            for j in (range(2) if not os.environ.get('Q_NO_GATE') else ()):
                g = gt[j]
                sub = m * 2 + j
                for h in range(8):
                    vec("dve", "tensor_tensor", r=[pg[j], elneg], w=[g], out=g.v[:, h, :], in0=pg[j].v[:, h * NB:(h + 1) * NB],
                        in1=elneg.v[:, m * NB:(m + 1) * NB], op=ALU.add)
                for h in range(8):
                    t8 = top8[h % 2]
                    vec("dve", "max", r=[g], w=[t8], out=t8.v, in_=g.v[:, h, :])
                    vec("dve", "scalar_tensor_tensor", r=[g, t8, el01], w=[maskg], out=maskg.v[:, sub, h, :], in0=g.v[:, h, :],
                        scalar=t8.v[:, 2:3], in1=el01.v[:, m * NB:(m + 1) * NB], op0=ALU.is_ge, op1=ALU.mult)
                    vec("dve", "tensor_tensor", r=[maskg, own01], w=[maskg], out=maskg.v[:, sub, h, :],
                        in0=maskg.v[:, sub, h, :], in1=own01.v[:, m * NB:(m + 1) * NB], op=ALU.add)
            pz = nxt(misc, "misc")
            for k in (range(8) if not os.environ.get('Q_NO_SEL') else ()):
                mm(pz.v[:, 0:8], selt.v[:, k, :], negc.v[:, 8 * m + k, :], k == 0, k == 7, r=[selt, negc], w=[pz])
            if not os.environ.get('Q_NO_SEL'):
                vec("dve", "tensor_copy", r=[pz], w=[znb], out=znb.v[:, m, :], in_=pz.v[:, 0:8])
        dbg_dump("maskg", maskg, [128, NO // 128, 8, NB])
        dbg_dump("kbarT", kbarT, [128, 8, NB])
        dbg_dump("negc", negc, [128, NT, 8])
        dbg_dump("znb", znb, [128, NBO, 8])
        P.barrier()
        A.off = mq2
        if stop == "Q":
            P.emit(block, sems)
            return nc

        NBmax = NB
        qtr = A.ring(2, [128, NO], BF16)
        ktr = A.ring(2, [128, NBmax * 256], BF16)
        vtr = A.ring(2, [128, 2 * NBmax, 129], BF16)
        bmr = A.ring(2, [128, 16, 256], BF16)
        ptr_ = A.ring(8, [128, 256], BF16)
        oacc = A.ring(2, [128, 2, 129], F32)
        badj = A.ring(2, [128, 2 * NBmax], F32)
        rcp = A.ring(2, [128, 2], F32)
        ohs = A.ring(2, [128, 2, 128], BF16)
        ost = A.ring(2, [128, 256], BF16)
        OTv = OTs.ap().rearrange("h p t -> p h t")
        stpB = stp[0:3]
        trB = trp[1]
        heads = [(k_, h_) for h_ in range(8) for k_ in ("a", "f")]
        loaded = {}
        hk = [0]; pc = [0]; ac = [0]; oc_ = [0]

        def load_head(idx):
            kind, h = heads[idx]
            sl = hk[0] % 2; hk[0] += 1
            k_t, v_t, q_t = ktr[sl], vtr[sl], qtr[sl]
            KT = KTa if kind == "a" else KTf
            VS = Va if kind == "a" else Vf
            hq = h if kind == "a" else 8 + h
            dma("sp", q_t.v, QTs.ap()[hq, :, :], w=[q_t])
            dma("sp", k_t.v, KT.ap()[h, :, :], w=[k_t])
            dma("sp", v_t.v, VS.ap()[h, :, :, :], w=[v_t])
            b_t = None
            if kind == "a":
                b_t = bmr[h % 2]
                for ni in range(8):
                    for half in range(2):
                        src = bass.AP(tensor=Fs, offset=h * 4096 + ni * 512 + 128 * (1 - half), ap=[[1, 128], [1, 256]])
                        dma("sp", b_t.v[:, ni * 2 + half, :], src, r=[Fs_t], w=[b_t])
            loaded[idx] = (k_t, v_t, b_t, q_t)

        load_head(0)
        load_head(1)
        items = []
        for idx, (kind, h) in enumerate(heads):
            for m in range(NBO):
                for st in range(2 * (4 * m + 4)):
                    items.append((idx, kind, h, m, st))
        state = {}

        def group_begin(idx, kind, h, m):
            ext = 4 * m + 4
            i2 = ac[0] % 2; ac[0] += 1
            if kind == "f":
                bj = badj[i2]
                vec("dve", "tensor_scalar", r=[negc, znb], w=[bj], out=bj.v[:, 0:2 * ext], in0=negc.v[:, 0:2 * ext, h],
                    scalar1=znb.v[:, m, h:h + 1], scalar2=None, op0=ALU.subtract)
                state[(idx, m)] = {"bj": bj, "fo": fop}
            else:
                state[(idx, m)] = {"oa": oacc[i2]}

        def qk(it):
            idx, kind, h, m, st = it
            k_t, v_t, b_t, q_t = loaded[idx]
            n = st // 2; half = st % 2
            ps = nxt(stpB, "st")
            extra_mm = None
            if kind == "a":
                ni = n - 4 * m + 4
                if ni >= 0:
                    extra_mm = (b_t.v[:, ni * 2 + half, :], b_t)
            else:
                ni = n - 4 * m
                if ni >= 0:
                    extra_mm = (fmask.v[:, ni * 2 + half, :], fmask)
            mm(ps.v, k_t.v[:, st * 128:(st + 1) * 128], q_t.v[:, m * 256:(m + 1) * 256], True, extra_mm is None,
               r=[k_t, q_t], w=[ps])
            if extra_mm is not None:
                mm(ps.v, identb.v, extra_mm[0], False, True, r=[identb, extra_mm[1]], w=[ps])
            p_t = ptr_[pc[0] % 8]; pc[0] += 1
            if kind == "a":
                if extra_mm is None:
                    act(p_t.v, ps.v, AF.Exp, r=[ps, rb31], w=[p_t], bias=rb31.v[:, h:h + 1], scale=SCALE)
                else:
                    act(p_t.v, ps.v, AF.Exp, r=[ps], w=[p_t], scale=SCALE)
            else:
                bj = state[(idx, m)]["bj"]
                act(p_t.v, ps.v, AF.Exp, r=[ps, bj], w=[p_t], bias=bj.v[:, st:st + 1], scale=SCALE)
            return p_t

        def finish(idx, kind, h, m, oh):
            hq = h if kind == "a" else 8 + h
            for sub in range(2):
                tr(trB.v[:, sub * 128:(sub + 1) * 128], oh.v[:, sub, :], identb.v, r=[oh, identb], w=[trB])
            os_ = ost[oc_[0] % 2]
            act(os_.v, trB.v[:, 0:256], AF.Copy, r=[trB], w=[os_])
            dma("sp", OTs.ap()[hq, :, m * 256:(m + 1) * 256], os_.v, r=[os_])

        def pv(it, p_t):
            idx, kind, h, m, st = it
            k_t, v_t, b_t, q_t = loaded[idx]
            ext = 4 * m + 4
            n = st // 2; half = st % 2
            last = st == 2 * ext - 1
            stt = state[(idx, m)]
            if kind == "a":
                if half == 0:
                    stt["p0"] = p_t
                else:
                    on = nxt(onp, "on")
                    p0 = stt["p0"]
                    for sub in range(2):
                        mm(on[sub].v, p0.v[:, sub * 128:(sub + 1) * 128], v_t.v[:, st - 1, :], True, False,
                           r=[p0, v_t], w=[on[sub]])
                        mm(on[sub].v, p_t.v[:, sub * 128:(sub + 1) * 128], v_t.v[:, st, :], False, True,
                           r=[p_t, v_t], w=[on[sub]])
                    oa = stt["oa"]
                    for sub in range(2):
                        msk = maskg.v[:, 2 * m + sub, h, n:n + 1]
                        if n == 0:
                            vec("dve", "tensor_scalar", r=[on[sub], maskg], w=[oa], out=oa.v[:, sub, :], in0=on[sub].v,
                                scalar1=msk, scalar2=None, op0=ALU.mult)
                        else:
                            vec("dve", "scalar_tensor_tensor", r=[on[sub], maskg, oa], w=[oa], out=oa.v[:, sub, :],
                                in0=on[sub].v, scalar=msk, in1=oa.v[:, sub, :], op0=ALU.mult, op1=ALU.add)
                if last:
                    oa = stt["oa"]
                    rc_ = rcp[oc_[0] % 2]; oh = ohs[oc_[0] % 2]
                    vec("dve", "reciprocal", r=[oa], w=[rc_], out=rc_.v, in_=oa.v[:, :, 128])
                    for sub in range(2):
                        vec("dve", "tensor_scalar", r=[oa, rc_], w=[oh], out=oh.v[:, sub, :], in0=oa.v[:, sub, 0:128],
                            scalar1=rc_.v[:, sub:sub + 1], scalar2=None, op0=ALU.mult)
                    finish(idx, kind, h, m, oh)
                    oc_[0] += 1
            else:
                fo = stt["fo"]
                for sub in range(2):
                    mm(fo[sub].v, p_t.v[:, sub * 128:(sub + 1) * 128], v_t.v[:, st, :], st == 0, last,
                       r=[p_t, v_t], w=[fo[sub]])
                if last:
                    rc_ = rcp[oc_[0] % 2]; oh = ohs[oc_[0] % 2]
                    for sub in range(2):
                        vec("dve", "reciprocal", r=[fo[sub]], w=[rc_], out=rc_.v[:, sub:sub + 1], in_=fo[sub].v[:, 128:129])
                        vec("dve", "tensor_scalar", r=[fo[sub], rc_], w=[oh], out=oh.v[:, sub, :], in0=fo[sub].v[:, 0:128],
                            scalar1=rc_.v[:, sub:sub + 1], scalar2=None, op0=ALU.mult)
                    finish(idx, kind, h, m, oh)
                    oc_[0] += 1

        LOOK = 2
        pend = []

        def retire():
            it0, p0_ = pend.pop(0)
            pv(it0, p0_)
            idx0, _, _, m0_, st0 = it0
            if m0_ == NBO - 1 and st0 == 2 * (4 * m0_ + 4) - 1 and idx0 + 2 < len(heads):
                load_head(idx0 + 2)

        for it in items:
            if it[4] == 0:
                group_begin(it[0], it[1], it[2], it[3])
            p_t = qk(it)
            pend.append((it, p_t))
            if len(pend) > LOOK:
                retire()
        while pend:
            retire()
        P.barrier()
        A.off = mq
        if stop == "B":
            P.emit(block, sems)
            return nc

        hres = A.alloc([128, 4, 2048], F32)
        aT = A.alloc([128, 16, 512], BF16)
        ot = A.alloc([128, 16, 512], BF16)
        mT = A.alloc([128, 16, 512], BF16)
        wsl = A.ring(3, [128, 8192], BF16)
        sar = A.ring(4, [128, 512], F32)
        sfr = A.ring(4, [128, 512], F32)
        rlr = A.ring(2, [128, 512], F32)
        qcT = A.alloc([128, 4, 512], BF16)
        ptc = A.ring(2, [128, 512], BF16)
        octm = A.alloc([128, 4, 512], BF16)
        ocT = A.alloc([128, 4, 512], BF16)
        memt = []
        for j in range(2):
            tt = T(mT.v[:, j * 8:(j + 1) * 8, :].rearrange("p a b -> p (a b)").bitcast(F32)); tt.b = mT.b
            memt.append(tt)
        hs = [T(hres.v[:, j, :]) for j in range(4)]
        atmC = list(atm)
        for src_t in (octm, ocT):
            t_ = T(src_t.v.rearrange("p a b -> p (a b)")); t_.b = src_t.b
            atmC.append(t_)

        def norm_tile(g):
            outs = []
            for j in range(4):
                outs.append(norm_a(hs[j], g, atmC))
            for j in range(4):
                norm_tr(outs[j], aT, j * 128)

        wc = [0]
        if os.environ.get('K_STATS'):
            print('ARENA phaseC off', A.off, 'of', A.size, 'pers_mark', pers_mark, flush=True)
        gc = [1]

        PRE = 2
        plan = []
        plan.append(([16, 512], [lambda t: load_w(t, w_ck, 0, 16, 0, 512)]))
        plan.append(([16, 512], [lambda t: load_w(t, w_cv, 0, 16, 0, 512)]))
        for _tc in range(NO // 512):
            for fg in range(4):
                plan.append(([16, 512], [lambda t, fg=fg: load_w(t, w_in, 0, 16, C_GA + fg * 512, 512)]))
                plan.append(([16, 512], [lambda t, fg=fg: load_w(t, w_in, 0, 16, C_GF + fg * 512, 512)]))
                plan.append(([16, 512], [lambda t, fg=fg: load_w(t, w_bm, 0, 8, fg * 512, 512, k0=0),
                                         lambda t, fg=fg: load_w(t, w_bf, 0, 8, fg * 512, 512, k0=8)]))
            for cg in range(4):
                plan.append(([16, 512], [lambda t, cg=cg: load_w(t, w_mix, 0, 16, cg * 512, 512)]))
            plan.append(([16, 512], [lambda t: load_w(t, w_cq, 0, 16, 0, 512)]))
            plan.append(([4, 2048], [lambda t: load_w(t, w_co, 0, 4, 0, 2048)]))
            for fg in range(NFG):
                for q4 in range(4):
                    plan.append(([16, 512], [lambda t, fg=fg, q4=q4: load_w(t, w_ff1, 0, 16, fg * 2048 + q4 * 512, 512)]))
                for cg in range(4):
                    plan.append(([16, 512], [lambda t, fg=fg, cg=cg: load_w(t, w_ff2, fg * 2048, 16, cg * 512, 512)]))
        wst = {"issued": 0, "taken": 0, "tiles": {}}

        def wslot(shape3):
            i = wst["taken"]
            while wst["issued"] < min(len(plan), i + 1 + PRE):
                k = wst["issued"]
                shp, loaders = plan[k]
                w = wsl[k % 3]
                t = T(w.v[:, 0:shp[0] * shp[1]].rearrange("p (a b) -> p a b", a=shp[0])); t.b = w.b
                for fn in loaders:
                    fn(t)
                wst["tiles"][k] = t
                wst["issued"] += 1
            assert plan[i][0] == list(shape3), (i, plan[i][0], shape3)
            wst["taken"] += 1
            return wst["tiles"].pop(i)

        def load_g(gd):
            g = gbr[0]
            dma("sp", g.v, gd[:, :], w=[g])
            return g

        def add_res(i, sub, cg, pm):
            vec("dve", "tensor_tensor", r=[pm, hs[sub]], w=[hs[sub]], out=hs[sub].v[:, cg * 512:(cg + 1) * 512], in0=pm.v,
                in1=hs[sub].v[:, cg * 512:(cg + 1) * 512], op=ALU.add)

        gm = load_g(gmem_d)
        vec("pool", "memset", w=[Vc], ap=Vc.v, constant=1.0)
        for j in range(2):
            dma("sp", memt[j].v, memb[j * 128:(j + 1) * 128, :], w=[memt[j]])
            norm_T(memt[j], gm, aT, j * 128)
        wk = wslot([16, 512])
        for hc in range(4):
            pm = nxt(mmp, "mm")
            for dc in range(16):
                mm(pm.v[:, 0:256], wk.v[:, dc, hc * 128:(hc + 1) * 128], aT.v[:, dc, 0:256], dc == 0, dc == 15, r=[wk, aT], w=[pm])
            copy_any(hc, KcT.v[:, hc, :], pm.v[:, 0:256], r=[pm], w=[KcT])
        wv = wslot([16, 512])
        for j in range(2):
            pm = nxt(mmp, "mm")
            for dc in range(16):
                mm(pm.v, aT.v[:, dc, j * 128:(j + 1) * 128], wv.v[:, dc, :], dc == 0, dc == 15, r=[wv, aT], w=[pm])
            copy_any(j, Vc.v[:, j, :, 0:128], pm.v.rearrange("p (a b) -> p a b", a=4), r=[pm], w=[Vc])

        out_dmas = []
        for tc_ in range(NO // 512):
            t0 = tc_ * 512
            g1 = load_g(gmix_d)
            for j in range(4):
                dma("sp", hs[j].v, xo[t0 + j * 128:t0 + (j + 1) * 128, :], w=[hs[j]])
            norm_tile(g1)
            dma("sp", ot.v, OTv[:, :, t0:t0 + 512], w=[ot])
            for fg in range(4):
                wga = wslot([16, 512])
                for f4 in range(4):
                    cs = slice(f4 * 128, (f4 + 1) * 128)
                    pga = nxt(mmp, "mm")
                    for dc in range(16):
                        mm(pga.v, wga.v[:, dc, cs], aT.v[:, dc, :], dc == 0, dc == 15, r=[wga, aT], w=[pga])
                    act(sar[f4].v, pga.v, AF.Sigmoid, r=[pga], w=[sar[f4]])
                wgf = wslot([16, 512])
                for f4 in range(4):
                    cs = slice(f4 * 128, (f4 + 1) * 128)
                    pgf = nxt(mmp, "mm")
                    for dc in range(16):
                        mm(pgf.v, wgf.v[:, dc, cs], aT.v[:, dc, :], dc == 0, dc == 15, r=[wgf, aT], w=[pgf])
                    act(sfr[f4].v, pgf.v, AF.Sigmoid, r=[pgf], w=[sfr[f4]])
                wb = wslot([16, 512])
                for f4 in range(4):
                    fc = fg * 4 + f4
                    cs = slice(f4 * 128, (f4 + 1) * 128)
                    pua = nxt(mmp, "mm")
                    for h in range(8):
                        mm(pua.v, wb.v[:, h, cs], ot.v[:, h, :], h == 0, h == 7, r=[wb, ot], w=[pua])
                    vec("dve", "tensor_tensor", r=[pua, sar[f4]], w=[sar[f4]], out=sar[f4].v, in0=pua.v, in1=sar[f4].v, op=ALU.mult)
                    puf = nxt(mmp, "mm")
                    for h in range(8):
                        mm(puf.v, wb.v[:, 8 + h, cs], ot.v[:, 8 + h, :], h == 0, h == 7, r=[wb, ot], w=[puf])
                    vec("dve", "tensor_tensor", r=[puf, sfr[f4]], w=[sfr[f4]], out=sfr[f4].v, in0=puf.v, in1=sfr[f4].v, op=ALU.mult)
                    vec("pool", "tensor_tensor", r=[sar[f4], sfr[f4]], w=[mT], out=mT.v[:, fc, :], in0=sar[f4].v, in1=sfr[f4].v, op=ALU.add)
            for cg in range(4):
                wm = wslot([16, 512])
                for sub in range(4):
                    pm = nxt(mmp, "mm")
                    for fc in range(16):
                        mm(pm.v, mT.v[:, fc, sub * 128:(sub + 1) * 128], wm.v[:, fc, :], fc == 0, fc == 15, r=[mT, wm], w=[pm])
                    add_res(0, sub, cg, pm)
            g2 = load_g(gcross_d)
            norm_tile(g2)
            wq = wslot([16, 512])
            for hc in range(4):
                pm = nxt(mmp, "mm")
                for dc in range(16):
                    mm(pm.v, wq.v[:, dc, hc * 128:(hc + 1) * 128], aT.v[:, dc, :], dc == 0, dc == 15, r=[wq, aT], w=[pm])
                copy_any(hc, qcT.v[:, hc, :], pm.v, r=[pm], w=[qcT])
            wo = wslot([4, 2048])
            for hc in range(4):
                pts = []
                for st in range(2):
                    ps = nxt(mmp, "mm")
                    mm(ps.v, KcT.v[:, hc, st * 128:(st + 1) * 128], qcT.v[:, hc, :], True, True, r=[KcT, qcT], w=[ps])
                    p_t = ptc[st]
                    act(p_t.v, ps.v, AF.Exp, r=[ps], w=[p_t], scale=SCALE)
                    pts.append(p_t)
                for sub in range(4):
                    po = misc[sub % 2]
                    for st in range(2):
                        mm(po.v[:, 0:129], pts[st].v[:, sub * 128:(sub + 1) * 128], Vc.v[:, st, hc, :], st == 0, st == 1,
                           r=[pts[st], Vc], w=[po])
                    rc_ = rstdr[rc["ss"] % 4]; rc["ss"] += 1
                    vec("dve", "reciprocal", r=[po], w=[rc_], out=rc_.v, in_=po.v[:, 128:129])
                    vec("dve", "tensor_scalar", r=[po, rc_], w=[octm], out=octm.v[:, sub, hc * 128:(hc + 1) * 128], in0=po.v[:, 0:128],
                        scalar1=rc_.v[:, 0:1], scalar2=None, op0=ALU.mult)
            for sub in range(4):
                pt = nxt(trp, "tr")
                for hc in range(4):
                    tr(pt.v[:, hc * 128:(hc + 1) * 128], octm.v[:, sub, hc * 128:(hc + 1) * 128], identb.v, r=[octm, identb], w=[pt])
                copy_any(sub, ocT.v[:, :, sub * 128:(sub + 1) * 128], pt.v[:, 0:512].rearrange("p (a b) -> p a b", a=4),
                         r=[pt], w=[ocT])
            for cg in range(4):
                for sub in range(4):
                    pm = nxt(mmp, "mm")
                    for hc in range(4):
                        mm(pm.v, ocT.v[:, hc, sub * 128:(sub + 1) * 128], wo.v[:, hc, cg * 512:(cg + 1) * 512], hc == 0, hc == 3,
                           r=[ocT, wo], w=[pm])
                    add_res(0, sub, cg, pm)
            g3 = load_g(gmlp_d)
            norm_tile(g3)
            for fg in range(NFG):
                hid = ot if fg % 2 == 0 else mT
                for q4 in range(4):
                    w1 = wslot([16, 512])
                    for f4 in range(4):
                        fc = q4 * 4 + f4
                        pm = nxt(mmp, "mm")
                        for dc in range(16):
                            mm(pm.v, w1.v[:, dc, f4 * 128:(f4 + 1) * 128], aT.v[:, dc, :], dc == 0, dc == 15, r=[w1, aT], w=[pm])
                        rl = rlr[fc % 2]
                        act(rl.v, pm.v, AF.Relu, r=[pm], w=[rl])
                        vec("pool", "tensor_tensor", r=[rl], w=[hid], out=hid.v[:, fc, :], in0=rl.v, in1=rl.v, op=ALU.mult)
                for cg in range(4):
                    w2 = wslot([16, 512])
                    for sub in range(4):
                        pm = nxt(mmp, "mm")
                        for fc in range(16):
                            mm(pm.v, hid.v[:, fc, sub * 128:(sub + 1) * 128], w2.v[:, fc, :], fc == 0, fc == 15, r=[hid, w2], w=[pm])
                        add_res(0, sub, cg, pm)
            g4 = load_g(gfin_d)
            for j in range(4):
                ss = ssr[rc["ss"] % 4]; rstd = rstdr[rc["ss"] % 4]; rc["ss"] += 1
                vec("pool", "memset", w=[ss], ap=ss.v, constant=0.0)
                act(sqj.v, hs[j].v, AF.Square, r=[hs[j]], w=[sqj, ss], accum_out=ss.v)
                act(rstd.v, ss.v, AF.Sqrt, r=[ss], w=[rstd], bias=EPS, scale=1.0 / D)
                vec("dve", "reciprocal", r=[rstd], w=[rstd], out=rstd.v, in_=rstd.v)
                vec("dve", "scalar_tensor_tensor", r=[hs[j], rstd, g4], w=[hs[j]], out=hs[j].v, in0=hs[j].v,
                    scalar=rstd.v[:, 0:1], in1=g4.v, op0=ALU.mult, op1=ALU.mult)
                out_dmas.append(dma("sp", out_d[t0 + j * 128:t0 + (j + 1) * 128, :], hs[j].v, r=[hs[j]]))
        P.add("sp", lambda: nc.sync.nop(), extra=out_dmas)
        P.barrier()
        P.emit(block, sems)
    return nc


def _t5_bucket_table(maxd):
    n = np.arange(maxd + 1)
    nf = np.maximum(n, 1).astype(np.float32)
    ratio = (np.log((nf / np.float32(16)).astype(np.float32)).astype(np.float32)
             / np.float32(math.log(1024 / 16))).astype(np.float32)
    large = 16 + (ratio * np.float32(16)).astype(np.float32).astype(np.int32)
    large = np.minimum(large, 31)
    return np.where(n < 16, n, large)


def _core_tables(r, NB, NBO):
    bt = _t5_bucket_table(8 * 256 + 512)
    oh = np.zeros((33, 8, 512), np.float32)
    u = np.arange(511)
    for ni in range(8):
        npr = ni - 4
        dist = (r - npr) * 256 + (u - 255)
        b = np.where(dist < 0, 32, bt[np.clip(dist, 0, len(bt) - 1)])
        oh[b, ni, u] = 1.0
    fm = np.zeros((128, 4, 2, 256), np.float32)
    s = np.arange(128)[:, None]
    q = np.arange(256)[None, :]
    for npr in range(4):
        for half in range(2):
            if npr < r:
                continue
            if npr > r:
                fm[:, npr, half, :] = NEG
            else:
                fm[:, npr, half, :] = np.where(half * 128 + s > q, NEG, 0.0)
    sel = np.zeros((128, 8, 128), np.float32)
    sel[0, 2 * r + 1, :] = 1.0
    elneg = np.zeros((NBO, NB), np.float32)
    el01 = np.zeros((NBO, NB), np.float32)
    own = np.zeros((NBO, NB), np.float32)
    for m in range(NBO):
        o = 4 * m + r
        el01[m, :o] = 1.0
        elneg[m, o:] = -1e30
        own[m, o] = 1.0
    bc = lambda a: np.ascontiguousarray(np.broadcast_to(a.reshape(1, -1), (128, a.size)))
    return dict(oh33=oh.reshape(33, 4096), fmask=fm.reshape(128, 2048), sel=sel.reshape(128, 1024),
                eligneg=bc(elneg), elig01=bc(el01), own01=bc(own))


def _prep(inputs, S):
    NB = S // 256
    NBO = NB // 4
    f = lambda a: np.ascontiguousarray(np.asarray(a, dtype=np.float32))
    x = f(inputs["x"]); mem = f(inputs["mem"])
    bc = lambda g: np.ascontiguousarray(np.broadcast_to(f(g).reshape(1, D), (128, D)))
    rel = f(inputs["rel_bias"])
    shared = dict(
        w_in=f(inputs["w_in"][0]), w_bm=f(inputs["w_branch_moba"][0]), w_bf=f(inputs["w_branch_fox"][0]),
        w_mix=f(inputs["w_mix_out"][0]), w_cq=f(inputs["w_cq"][0]), w_ck=f(inputs["w_ck"][0]), w_cv=f(inputs["w_cv"][0]),
        w_co=f(inputs["w_co"][0]), w_ff1=f(inputs["w_ff1"][0]), w_ff2=f(inputs["w_ff2"][0]),
        g_mix_b=bc(inputs["g_mix"][0]), g_cross_b=bc(inputs["g_cross"][0]), g_mem_b=bc(inputs["g_mem"][0]),
        g_mlp_b=bc(inputs["g_mlp"][0]), g_final_b=bc(inputs["g_final"]),
        b_forget_c=f(inputs["b_forget"][0]).reshape(8, 1), rbT=np.ascontiguousarray(rel.T),
        rb31b=np.ascontiguousarray(np.broadcast_to(rel[:, 31].reshape(1, 8), (128, 8))),
        ident=np.eye(128, dtype=np.float32),
    )
    tabs = [_core_tables(r, NB, NBO) for r in range(4)]
    in_maps = []
    rows = []
    for c in range(8):
        b, r = c // 4, c % 4
        xbb = x[b]
        xbr = np.ascontiguousarray(xbb.reshape(S // 128, 128, D)[:, ::-1, :].reshape(S, D))
        idx = np.concatenate([np.arange((4 * m + r) * 256, (4 * m + r + 1) * 256) for m in range(NBO)])
        rows.append((b, idx))
        mp = dict(shared)
        mp.update(tabs[r])
        mp.update(xb=xbb, xbr=xbr, xo=np.ascontiguousarray(xbb[idx]), memb=mem[b])
        in_maps.append(mp)
    return in_maps, rows


_NC_CACHE = {}


def _run(inputs, S, DFF, dbg=False, stop=None):
    key = (S, DFF, dbg)
    if key not in _NC_CACHE:
        _NC_CACHE[key] = build(S, DFF, dbg, stop)
    nc = _NC_CACHE[key]
    in_maps, rows = _prep(inputs, S)
    ncr = int(os.environ.get("K_NCORES", "8"))
    res = run_bass_kernel_spmd(nc, in_maps[:ncr], core_ids=list(range(ncr)))
    out = np.zeros((2, S, D), np.float32)
    for c in range(ncr):
        b, idx = rows[c]
        out[b, idx] = res.results[c]["out"]
    if dbg:
        return out, res
    return out


def kernel(**inputs):
    return _run(inputs, 8192, 8192)
```

```python
import math
import os
from contextlib import ExitStack
import numpy as np
import concourse.bass as bass
import concourse.mybir as mybir
from concourse.bass_utils import run_bass_kernel_spmd

F32 = mybir.dt.float32
BF16 = mybir.dt.bfloat16
U8 = mybir.dt.uint8
AF = mybir.ActivationFunctionType
ALU = mybir.AluOpType
AX = mybir.AxisListType

ENGS = ("pe", "act", "dve", "pool", "sp")

D = 2048
DC = 16
NH = 8
C_QA, C_KA, C_VA, C_QF, C_KF, C_VF, C_FL, C_GA, C_GF = 0, 1024, 2048, 3072, 4096, 5120, 6144, 6152, 8200
IN_W = 10248
SCALE = 128 ** -0.5
EPS = 1e-6
NEG = -30000.0


class Buf:
    __slots__ = ("writers", "readers")

    def __init__(self):
        self.writers = []
        self.readers = {}


class Ins:
    __slots__ = ("eng", "fn", "deps", "is_dma", "marked", "sem", "val")

    def __init__(self, eng, fn, is_dma):
        self.eng = eng
        self.fn = fn
        self.deps = []
        self.is_dma = is_dma
        self.marked = False
        self.sem = None
        self.val = 0


class Prog:
    N_DMA_SEMS = 14

    def __init__(self, nc):
        self.nc = nc
        self.q = {e: [] for e in ENGS}
        self.last = {e: None for e in ENGS}
        self.dmas = []

    def add(self, eng, fn, r=(), w=(), is_dma=False, extra=()):
        ins = Ins(eng, fn, is_dma)
        deps = list(extra)
        for b in r:
            deps.extend(b.writers)
        for b in w:
            deps.extend(b.writers)
            deps.extend(b.readers.values())
        seen = set()
        for d in deps:
            if d is ins or id(d) in seen:
                continue
            seen.add(id(d))
            if (not d.is_dma) and (not is_dma) and d.eng == eng and eng == "pe":
                continue
            ins.deps.append(d)
            d.marked = True
        for b in r:
            key = ("dma", id(ins)) if is_dma else eng
            b.readers[key] = ins
        for b in w:
            b.writers = [ins]
            b.readers = {}
        self.q[eng].append(ins)
        if is_dma:
            self.dmas.append(ins)
        else:
            self.last[eng] = ins
        return ins

    def barrier(self):
        nc = self.nc
        eo = {"pe": nc.tensor, "act": nc.scalar, "dve": nc.vector, "pool": nc.gpsimd, "sp": nc.sync}
        lasts = [v for v in self.last.values() if v is not None]
        dm = list(self.dmas)
        self.dmas = []
        for e in ENGS:
            self.add(e, (lambda o=eo[e]: o.nop()), extra=[x for x in lasts if x.eng != e] + dm)

    def emit(self, block, sems):
        for e in ENGS:
            cnt = 0
            dma_cnt = [0] * self.N_DMA_SEMS
            last_on = [None] * self.N_DMA_SEMS
            rr = 0
            for ins in self.q[e]:
                if ins.is_dma:
                    k = rr % self.N_DMA_SEMS
                    rr += 1
                    dma_cnt[k] += 16
                    ins.sem = sems["dma_" + e][k]
                    ins.val = dma_cnt[k]
                    if last_on[k] is not None:
                        ins.deps.append(last_on[k])
                    last_on[k] = ins
                elif ins.marked:
                    cnt += 1
                    ins.sem = sems[e]
                    ins.val = cnt
            if os.environ.get("K_STATS"):
                print("STATS", e, "n_ins", len(self.q[e]), "marked", cnt, "dma_max", max(dma_cnt), flush=True)
        nc = self.nc
        engobj = {"pe": nc.tensor, "act": nc.scalar, "dve": nc.vector, "pool": nc.gpsimd, "sp": nc.sync}

        def run(e):
            def body(_eng):
                eo = engobj[e]
                waited = {}
                for ins in self.q[e]:
                    need = {}
                    for d in ins.deps:
                        s = d.sem
                        if d.val > need.get(id(s), (None, 0))[1]:
                            need[id(s)] = (s, d.val)
                    for sid, (s, v) in need.items():
                        if waited.get(sid, 0) < v:
                            eo.wait_ge(s, v)
                            waited[sid] = v
                    bi = ins.fn()
                    if ins.is_dma:
                        bi.then_inc(ins.sem, 16)
                    elif ins.marked:
                        bi.then_inc(ins.sem, 1)
            return body

        block.tensor(run("pe"))
        block.scalar(run("act"))
        block.vector(run("dve"))
        block.gpsimd(run("pool"))
        block.sync(run("sp"))


class T:
    __slots__ = ("v", "b")

    def __init__(self, v):
        self.v = v
        self.b = Buf()


class Arena:
    def __init__(self, t, size):
        self.t = t
        self.size = size
        self.off = 0

    def alloc(self, shape, dt):
        esz = 4 if dt == F32 else 2
        n = int(np.prod(shape[1:])) * esz
        assert self.off + n <= self.size, ("sbuf arena overflow", self.off, n, self.size)
        v = self.t[0:shape[0], self.off:self.off + n].bitcast(dt)
        self.off += (n + 63) // 64 * 64
        if len(shape) == 3:
            v = v.rearrange("p (a b) -> p a b", a=shape[1])
        elif len(shape) == 4:
            v = v.rearrange("p (a b c) -> p a b c", a=shape[1], b=shape[2])
        return T(v)

    def ring(self, n, shape, dt):
        return [self.alloc(shape, dt) for _ in range(n)]


ARENA_BYTES = 212800


def build(S, DFF, dbg=False, stop=None):
    NB = S // 256
    NBO = NB // 4
    NO = NBO * 256
    NT = S // 128
    NFG = DFF // 2048
    nc = bass.Bass("TRN2", target_bir_lowering=False)

    def din(name, shape):
        return nc.dram_tensor(name, list(shape), F32, kind="ExternalInput").ap()

    kscr = "ExternalOutput" if dbg else "Internal"

    def dscr(name, shape, dt=BF16):
        return nc.dram_tensor(name, list(shape), dt, kind=kscr)

    xb = din("xb", [S, D]); xbr = din("xbr", [S, D]); xo = din("xo", [NO, D]); memb = din("memb", [256, D])
    w_in = din("w_in", [D, IN_W]); w_bm = din("w_bm", [1024, D]); w_bf = din("w_bf", [1024, D])
    w_mix = din("w_mix", [D, D]); w_cq = din("w_cq", [D, 512]); w_ck = din("w_ck", [D, 512])
    w_cv = din("w_cv", [D, 512]); w_co = din("w_co", [512, D]); w_ff1 = din("w_ff1", [D, DFF])
    w_ff2 = din("w_ff2", [DFF, D])
    gmix_d = din("g_mix_b", [128, D]); gcross_d = din("g_cross_b", [128, D]); gmem_d = din("g_mem_b", [128, D])
    gmlp_d = din("g_mlp_b", [128, D]); gfin_d = din("g_final_b", [128, D])
    bf_d = din("b_forget_c", [8, 1]); rbT_d = din("rbT", [32, 8]); rb31_d = din("rb31b", [128, 8])
    oh_d = din("oh33", [33, 4096]); fmask_d = din("fmask", [128, 2048]); sel_d = din("sel", [128, 1024])
    elneg_d = din("eligneg", [128, NBO * NB]); el01_d = din("elig01", [128, NBO * NB])
    own01_d = din("own01", [128, NBO * NB]); ident_d = din("ident", [128, 128])
    out_d = nc.dram_tensor("out", [NO, D], F32, kind="ExternalOutput").ap()

    KTf = dscr("KTf", [8, 128, S]); KTa = dscr("KTa", [8, 128, S])
    Vf = dscr("Vf", [8, 128, NT, 129]); Va = dscr("Va", [8, 128, NT, 129])
    QTs = dscr("QTs", [16, 128, NO])
    OTs = dscr("OTs", [16, 128, NO])
    Fs = dscr("Fs", [8, 4096])

    es = ExitStack()
    with es:
        arena_t = es.enter_context(nc.sbuf_tensor("arena", [128, ARENA_BYTES], U8))
        A = Arena(arena_t, ARENA_BYTES)
        pbank = [es.enter_context(nc.psum_tensor(f"pb{i}", [128, 512], F32)) for i in range(8)]
        sems = {}
        for e in ENGS:
            sems[e] = es.enter_context(nc.semaphore("s_" + e))
        for e in ("sp", "pool", "act"):
            sems["dma_" + e] = [es.enter_context(nc.semaphore(f"d_{e}{k}")) for k in range(Prog.N_DMA_SEMS)]
        block = es.enter_context(nc.Block())
        P = Prog(nc)

        def bufs(ts):
            return [t.b for t in ts]

        def dma(eng, out, in_, r=(), w=()):
            eo = {"sp": nc.sync, "pool": nc.gpsimd, "act": nc.scalar}[eng]
            return P.add(eng, lambda: eo.dma_start(out=out, in_=in_), r=bufs(r), w=bufs(w), is_dma=True)

        def mm(out, lhsT, rhs, start, stop, r=(), w=()):
            return P.add("pe", lambda: nc.tensor.matmul(out, lhsT=lhsT, rhs=rhs, start=start, stop=stop),
                         r=bufs(r), w=bufs(w))

        def tr(out, in_, ident, r=(), w=()):
            return P.add("pe", lambda: nc.tensor.transpose(out=out, in_=in_, identity=ident), r=bufs(r), w=bufs(w))

        def act(out, in_, func, r=(), w=(), **kw):
            return P.add("act", lambda: nc.scalar.activation(out=out, in_=in_, func=func, **kw), r=bufs(r), w=bufs(w))

        def vec(eng, name, r=(), w=(), **kw):
            eo = {"dve": nc.vector, "pool": nc.gpsimd}[eng]
            return P.add(eng, lambda: getattr(eo, name)(**kw), r=bufs(r), w=bufs(w))

        def copy_any(i, out, in_, r=(), w=()):
            if i % 2 == 0:
                return act(out, in_, AF.Copy, r=r, w=w)
            return vec("dve", "tensor_copy", r=r, w=w, out=out, in_=in_)

        def dbg_dump(name, t, shape):
            if not dbg:
                return
            dd = nc.dram_tensor("dbg_" + name, list(shape), F32, kind="ExternalOutput").ap()
            if len(shape) == 3:
                dma("sp", dd[:, :, :], t.v, r=[t])
            elif len(shape) == 4:
                dma("sp", dd[:, :, :, :], t.v, r=[t])
            else:
                dma("sp", dd[:, :], t.v, r=[t])

        class PS:
            pass
        mmp = [T(pbank[i][:, :]) for i in range(4)]
        trp = [T(pbank[4 + i][:, :].bitcast(BF16)) for i in range(2)]
        misc = [T(pbank[6 + i][:, :]) for i in range(2)]
        stp = [T(pbank[i][:, 0:256]) for i in range(2)]
        for i in range(2):
            t_ = T(pbank[4 + i][:, 0:256]); t_.b = trp[i].b
            stp.append(t_)
        onp = []
        for bk in range(2):
            pair = [T(pbank[2 + bk][:, sub * 256:sub * 256 + 129]) for sub in range(2)]
            pair[1].b = pair[0].b
            onp.append(pair)
        fop = [T(pbank[6][:, 0:129]), T(pbank[7][:, 0:129])]
        cnt = {"mm": 0, "tr": 0, "misc": 0, "st": 0, "on": 0, "ev": 0}

        def nxt(pool, key):
            i = cnt[key]
            cnt[key] += 1
            return pool[i % len(pool)]

        identb = A.alloc([128, 128], BF16); ident32 = A.alloc([128, 128], F32)
        negc = A.alloc([128, NT, 8], F32)
        kbarT = A.alloc([128, 8, NB], F32)
        elneg = A.alloc([128, NBO * NB], F32); el01 = A.alloc([128, NBO * NB], F32); own01 = A.alloc([128, NBO * NB], F32)
        rb31 = A.alloc([128, 8], F32)
        fmask = A.alloc([128, 8, 256], BF16)
        KcT = A.alloc([128, 4, 256], BF16); Vc = A.alloc([128, 2, 4, 129], BF16)
        znb = A.alloc([128, NBO, 8], F32)
        negb = A.alloc([8, 1], F32); ones8 = A.alloc([8, 512], F32)
        selt = A.alloc([128, 8, 128], F32)
        ssr = A.ring(4, [128, 1], F32); rstdr = A.ring(4, [128, 1], F32)
        sqj = A.alloc([128, 2048], BF16)
        gbr = A.ring(1, [128, 2048], F32)
        atm = A.ring(2, [128, 2048], BF16)
        pers_mark = A.off
        rc = {"ss": 0, "atm": 0, "xin": 0}

        dma("sp", ident32.v, ident_d[:, :], w=[ident32])
        vec("dve", "tensor_copy", r=[ident32], w=[identb], out=identb.v, in_=ident32.v)
        dma("sp", elneg.v, elneg_d[:, :], w=[elneg]); dma("sp", el01.v, el01_d[:, :], w=[el01])
        dma("sp", own01.v, own01_d[:, :], w=[own01]); dma("sp", rb31.v, rb31_d[:, :], w=[rb31])
        dma("pool", fmask.v, fmask_d.rearrange("p (a b) -> p a b", a=8), w=[fmask])
        dma("sp", selt.v, sel_d.rearrange("p (a b) -> p a b", a=8), w=[selt])
        dma("sp", negb.v, bf_d[:, :], w=[negb])
        vec("dve", "tensor_scalar", r=[negb], w=[negb], out=negb.v, in0=negb.v, scalar1=-1.0, scalar2=None, op0=ALU.mult)
        vec("pool", "memset", w=[ones8], ap=ones8.v, constant=1.0)

        def norm_a(xt, gb, ring=None):
            ring = ring or atm
            ss = ssr[rc["ss"] % 4]; rstd = rstdr[rc["ss"] % 4]; rc["ss"] += 1
            a = ring[rc["atm"] % len(ring)]; rc["atm"] += 1
            vec("pool", "memset", w=[ss], ap=ss.v, constant=0.0)
            act(sqj.v, xt.v, AF.Square, r=[xt], w=[sqj, ss], accum_out=ss.v)
            act(rstd.v, ss.v, AF.Sqrt, r=[ss], w=[rstd], bias=EPS, scale=1.0 / D)
            vec("dve", "reciprocal", r=[rstd], w=[rstd], out=rstd.v, in_=rstd.v)
            vec("dve", "scalar_tensor_tensor", r=[xt, rstd, gb], w=[a], out=a.v, in0=xt.v, scalar=rstd.v[:, 0:1],
                in1=gb.v, op0=ALU.mult, op1=ALU.mult)
            return a

        def norm_tr(a, dst, c0):
            for g8 in range(2):
                pt = nxt(trp, "tr")
                for k in range(8):
                    dc = g8 * 8 + k
                    tr(pt.v[:, k * 128:(k + 1) * 128], a.v[:, dc * 128:(dc + 1) * 128], identb.v, r=[a, identb], w=[pt])
                copy_any(cnt["ev"], dst.v[:, g8 * 8:(g8 + 1) * 8, c0:c0 + 128],
                         pt.v.rearrange("p (a b) -> p a b", a=8), r=[pt], w=[dst])
                cnt["ev"] += 1

        def norm_T(xt, gb, dst, c0):
            norm_tr(norm_a(xt, gb), dst, c0)

        def load_w(dst, wd, r0, kc, c0, ncols, k0=0, col_off=0):
            src = wd[r0:r0 + kc * 128, c0:c0 + ncols].rearrange("(c p) n -> p c n", p=128)
            step = max(1, min(kc, 4096 // ncols))
            for k in range(0, kc, step):
                kk = min(step, kc - k)
                dma("pool", dst.v[:, k0 + k:k0 + k + kk, col_off:col_off + ncols], src[:, k:k + kk, :], w=[dst])

        m0 = A.off
        rbx = A.alloc([33, 8], F32); oht = A.alloc([33, 4096], F32); fst = A.alloc([8, 4096], BF16)
        vec("pool", "memset", w=[rbx], ap=rbx.v[32:33, :], constant=NEG)
        dma("sp", rbx.v[0:32, :], rbT_d[:, :], w=[rbx])
        dma("sp", oht.v, oh_d[:, :], w=[oht])
        for k in range(8):
            pm = nxt(misc, "misc")
            mm(pm.v[0:8, :], rbx.v, oht.v[:, k * 512:(k + 1) * 512], True, True, r=[rbx, oht], w=[pm])
            act(fst.v[:, k * 512:(k + 1) * 512], pm.v[0:8, :], AF.Copy, r=[pm], w=[fst], scale=1.0 / SCALE)
        Fs_t = T(None)
        dma("sp", Fs.ap()[:, :], fst.v, r=[fst], w=[Fs_t])
        P.barrier()
        A.off = m0
        if stop == "t5":
            P.emit(block, sems)
            return nc

        def phase_A(moba):
            m0 = A.off
            W = A.alloc([128, 16, 2048], BF16)
            Wf = A.alloc([128, 16, 8], BF16)
            xin = A.ring(3, [128, 2048], F32)
            aT = A.ring(2, [128, 16, 512], BF16)
            kst = A.ring(4, [128, 512], BF16)
            vst = A.ring(2, [128, 8, 129], BF16)
            et = A.alloc([8, 512], F32); lt = A.alloc([8, 512], F32); nct = A.ring(2, [8, 512], F32)
            gb = gbr[0]
            dma("sp", gb.v, gmix_d[:, :], w=[gb])
            load_w(W, w_in, 0, 16, C_KA if moba else C_KF, 1024)
            load_w(W, w_in, 0, 16, C_VA if moba else C_VF, 1024, col_off=1024)
            if not moba:
                load_w(Wf, w_in, 0, 16, C_FL, 8)
            for v in vst:
                vec("pool", "memset", w=[v], ap=v.v, constant=1.0)
            if moba:
                vec("pool", "memset", w=[kbarT], ap=kbarT.v, constant=0.0)
            xsrc = xbr if moba else xb
            KT = KTa if moba else KTf
            VS = Va if moba else Vf
            VSv = VS.ap().rearrange("h p t c -> p h t c")
            atmA = A.ring(5, [128, 2048], BF16)

            def do_norm_a(t):
                outs = []
                for j in range(4):
                    xt = xin[rc["xin"] % 3]; rc["xin"] += 1
                    r0 = t * 512 + j * 128
                    dma("sp", xt.v, xsrc[r0:r0 + 128, :], w=[xt])
                    outs.append(norm_a(xt, gb, atmA))
                return outs

            def do_norm_tr(t, outs):
                for j in range(4):
                    norm_tr(outs[j], aT[t % 2], j * 128)

            do_norm_tr(0, do_norm_a(0))
            for t in range(S // 512):
                a_t = aT[t % 2]
                nxt_a = do_norm_a(t + 1) if t + 1 < S // 512 else None
                for h in range(8):
                    pm = nxt(mmp, "mm")
                    for dc in range(16):
                        mm(pm.v, W.v[:, dc, h * 128:(h + 1) * 128], a_t.v[:, dc, :], dc == 0, dc == 15,
                           r=[W, a_t], w=[pm])
                    ks = kst[(t * 8 + h) % 4]
                    if moba:
                        for i2 in range(2):
                            act(ks.v[:, i2 * 256:(i2 + 1) * 256], pm.v[:, i2 * 256:(i2 + 1) * 256], AF.Copy, r=[pm], w=[ks, kbarT],
                                accum_out=kbarT.v[:, h, 2 * t + i2:2 * t + i2 + 1])
                    else:
                        copy_any(cnt["ev"], ks.v, pm.v, r=[pm], w=[ks]); cnt["ev"] += 1
                    dma("sp", KT.ap()[h, :, t * 512:(t + 1) * 512], ks.v, r=[ks])
                if nxt_a is not None:
                    do_norm_tr(t + 1, nxt_a)
                for j in range(4):
                    vs = vst[(t * 4 + j) % 2]
                    for g in range(2):
                        pm = nxt(mmp, "mm")
                        for dc in range(16):
                            mm(pm.v, a_t.v[:, dc, j * 128:(j + 1) * 128], W.v[:, dc, 1024 + g * 512:1024 + (g + 1) * 512],
                               dc == 0, dc == 15, r=[W, a_t], w=[pm])
                        copy_any(cnt["ev"], vs.v[:, g * 4:(g + 1) * 4, 0:128], pm.v.rearrange("p (a b) -> p a b", a=4),
                                 r=[pm], w=[vs]); cnt["ev"] += 1
                    dma("sp", VSv[:, :, t * 4 + j, :], vs.v, r=[vs])
                if not moba:
                    pm = nxt(misc, "misc")
                    for dc in range(16):
                        mm(pm.v[0:8, :], Wf.v[:, dc, :], a_t.v[:, dc, :], dc == 0, dc == 15, r=[Wf, a_t], w=[pm])
                    act(et.v, pm.v[0:8, :], AF.Exp, r=[pm, negb], w=[et], bias=negb.v[:, 0:1], scale=-1.0)
                    act(lt.v, et.v, AF.Ln, r=[et], w=[lt], bias=1.0)
                    ncur = nct[t % 2]; nprev = nct[(t + 1) % 2]
                    if t == 0:
                        vec("dve", "tensor_tensor_scan", r=[ones8, lt], w=[ncur], out=ncur.v, data0=ones8.v, data1=lt.v,
                            initial=0.0, op0=ALU.mult, op1=ALU.add)
                    else:
                        vec("dve", "tensor_tensor_scan", r=[ones8, lt, nprev], w=[ncur], out=ncur.v, data0=ones8.v,
                            data1=lt.v, initial=nprev.v[:, 511:512], op0=ALU.mult, op1=ALU.add)
                    pm2 = nxt(misc, "misc")
                    for j in range(4):
                        tr(pm2.v[:, j * 8:(j + 1) * 8], ncur.v[:, j * 128:(j + 1) * 128], ident32.v[0:8, 0:8],
                           r=[ncur, ident32], w=[pm2])
                    vec("dve", "tensor_copy", r=[pm2], w=[negc], out=negc.v[:, 4 * t:4 * t + 4, :],
                        in_=pm2.v[:, 0:32].rearrange("p (a b) -> p a b", a=4))
            P.barrier()
            A.off = m0

        phase_A(False)
        if stop == "A1":
            P.emit(block, sems)
            return nc
        phase_A(True)
        if stop == "A2":
            P.emit(block, sems)
            return nc

        mq = A.off
        maskg = A.alloc([128, NO // 128, 8, NB], F32)
        mq2 = A.off
        W = A.alloc([128, 16, 2048], BF16)
        xin = A.ring(3, [128, 2048], F32)
        aT = A.ring(2, [128, 16, 256], BF16)
        qst = A.ring(2, [128, 16, 256], BF16)
        q32 = A.ring(2, [128, 256], F32)
        gt = A.ring(2, [128, 8, NB], F32)
        top8 = A.ring(2, [128, 8], F32)
        atmQ = A.ring(4, [128, 2048], BF16)
        gb = gbr[0]
        dma("sp", gb.v, gmix_d[:, :], w=[gb])
        load_w(W, w_in, 0, 16, C_QA, 1024)
        load_w(W, w_in, 0, 16, C_QF, 1024, col_off=1024)
        QCUT = int(os.environ.get("Q_CUT", "99"))
        QGM = int(os.environ.get('Q_GM', '1'))
        for m in range(min(NBO, int(os.environ.get('Q_SLOTS', '99'))) if QCUT >= 2 else 0):
            a_t = aT[m % 2]; qs = qst[m % 2]

            def do_norm_q_a(mm_):
                outs = []
                for j in range(2):
                    xt = xin[rc["xin"] % 3]; rc["xin"] += 1
                    r0 = mm_ * 256 + j * 128
                    dma("sp", xt.v, xo[r0:r0 + 128, :], w=[xt])
                    outs.append(norm_a(xt, gb, atmQ))
                return outs

            def do_norm_q_tr(mm_, outs):
                for j in range(2):
                    norm_tr(outs[j], aT[mm_ % 2], j * 128)

            if m == 0:
                do_norm_q_tr(0, do_norm_q_a(0))
            nxt_q = do_norm_q_a(m + 1) if m + 1 < NBO else None
            pg = [nxt(misc, "misc"), nxt(misc, "misc")]
            for h in range(16 if QCUT >= 3 else 0):
                pm = nxt(mmp, "mm")
                for dc in range(16):
                    mm(pm.v[:, 0:256], W.v[:, dc, h * 128:(h + 1) * 128], a_t.v[:, dc, :], dc == 0, dc == 15,
                       r=[W, a_t], w=[pm])
                copy_any(cnt["ev"], qs.v[:, h, :], pm.v[:, 0:256], r=[pm], w=[qs]); cnt["ev"] += 1
                if h == 7 and nxt_q is not None:
                    do_norm_q_tr(m + 1, nxt_q)
                if h < 8 and QCUT >= 4:
                    qq = q32[h % 2]
                    QQ = int(os.environ.get('Q_QQ', '6'))
                    if QQ == 0:
                        vec("dve", "tensor_copy", r=[pm], w=[qq], out=qq.v, in_=pm.v[:, 0:256])
                    elif QQ == 1:
                        act(qq.v, pm.v[:, 0:256], AF.Copy, r=[pm], w=[qq])
                    elif QQ == 2:
                        vec("dve", "tensor_copy", r=[qs], w=[qq], out=qq.v, in_=qs.v[:, h, :])
                    elif QQ == 6:
                        vec("dve", "tensor_copy", r=[pm, qs], w=[qq], out=qq.v, in_=pm.v[:, 0:256])
                    elif QQ == 3:
                        vec("pool", "tensor_copy", r=[qs], w=[qq], out=qq.v, in_=qs.v[:, h, :])
                    for j in range(2 if QGM else 0):
                        rhs_ = kbarT.v[:, h, :] if QGM == 1 else ident32.v[:, 0:NB]
                        mm(pg[j].v[:, h * NB:(h + 1) * NB], qq.v[:, j * 128:(j + 1) * 128], rhs_, True, True,
                           r=[qq, kbarT, ident32], w=[pg[j]])
            if QCUT < 5:
                continue
            dma("sp", QTs.ap().rearrange("h p t -> p h t")[:, :, m * 256:(m + 1) * 256], qs.v, r=[qs])
            for j in (range(2) if not os.environ.get('Q_NO_GATE') else ()):
                g = gt[j]
                sub = m * 2 + j
                for h in range(8):
                    vec("dve", "tensor_tensor", r=[pg[j], elneg], w=[g], out=g.v[:, h, :], in0=pg[j].v[:, h * NB:(h + 1) * NB],
                        in1=elneg.v[:, m * NB:(m + 1) * NB], op=ALU.add)
                for h in range(8):
                    t8 = top8[h % 2]
                    vec("dve", "max", r=[g], w=[t8], out=t8.v, in_=g.v[:, h, :])
                    vec("dve", "scalar_tensor_tensor", r=[g, t8, el01], w=[maskg], out=maskg.v[:, sub, h, :], in0=g.v[:, h, :],
                        scalar=t8.v[:, 2:3], in1=el01.v[:, m * NB:(m + 1) * NB], op0=ALU.is_ge, op1=ALU.mult)
                    vec("dve", "tensor_tensor", r=[maskg, own01], w=[maskg], out=maskg.v[:, sub, h, :],
                        in0=maskg.v[:, sub, h, :], in1=own01.v[:, m * NB:(m + 1) * NB], op=ALU.add)
            pz = nxt(misc, "misc")
            for k in (range(8) if not os.environ.get('Q_NO_SEL') else ()):
                mm(pz.v[:, 0:8], selt.v[:, k, :], negc.v[:, 8 * m + k, :], k == 0, k == 7, r=[selt, negc], w=[pz])
            if not os.environ.get('Q_NO_SEL'):
                vec("dve", "tensor_copy", r=[pz], w=[znb], out=znb.v[:, m, :], in_=pz.v[:, 0:8])
        dbg_dump("maskg", maskg, [128, NO // 128, 8, NB])
        dbg_dump("kbarT", kbarT, [128, 8, NB])
        dbg_dump("negc", negc, [128, NT, 8])
        dbg_dump("znb", znb, [128, NBO, 8])
        P.barrier()
        A.off = mq2
        if stop == "Q":
            P.emit(block, sems)
            return nc

        NBmax = NB
        qtr = A.ring(2, [128, NO], BF16)
        ktr = A.ring(2, [128, NBmax * 256], BF16)
        vtr = A.ring(2, [128, 2 * NBmax, 129], BF16)
        bmr = A.ring(2, [128, 16, 256], BF16)
        ptr_ = A.ring(8, [128, 256], BF16)
        oacc = A.ring(2, [128, 2, 129], F32)
        badj = A.ring(2, [128, 2 * NBmax], F32)
        rcp = A.ring(2, [128, 2], F32)
        ohs = A.ring(2, [128, 2, 128], BF16)
        ost = A.ring(2, [128, 256], BF16)
        OTv = OTs.ap().rearrange("h p t -> p h t")
        stpB = stp[0:3]
        trB = trp[1]
        heads = [(k_, h_) for h_ in range(8) for k_ in ("a", "f")]
        loaded = {}
        hk = [0]; pc = [0]; ac = [0]; oc_ = [0]

        def load_head(idx):
            kind, h = heads[idx]
            sl = hk[0] % 2; hk[0] += 1
            k_t, v_t, q_t = ktr[sl], vtr[sl], qtr[sl]
            KT = KTa if kind == "a" else KTf
            VS = Va if kind == "a" else Vf
            hq = h if kind == "a" else 8 + h
            dma("sp", q_t.v, QTs.ap()[hq, :, :], w=[q_t])
            dma("sp", k_t.v, KT.ap()[h, :, :], w=[k_t])
            dma("sp", v_t.v, VS.ap()[h, :, :, :], w=[v_t])
            b_t = None
            if kind == "a":
                b_t = bmr[h % 2]
                for ni in range(8):
                    for half in range(2):
                        src = bass.AP(tensor=Fs, offset=h * 4096 + ni * 512 + 128 * (1 - half), ap=[[1, 128], [1, 256]])
                        dma("sp", b_t.v[:, ni * 2 + half, :], src, r=[Fs_t], w=[b_t])
            loaded[idx] = (k_t, v_t, b_t, q_t)

        load_head(0)
        load_head(1)
        items = []
        for idx, (kind, h) in enumerate(heads):
            for m in range(NBO):
                for st in range(2 * (4 * m + 4)):
                    items.append((idx, kind, h, m, st))
        state = {}

        def group_begin(idx, kind, h, m):
            ext = 4 * m + 4
            i2 = ac[0] % 2; ac[0] += 1
            if kind == "f":
                bj = badj[i2]
                vec("dve", "tensor_scalar", r=[negc, znb], w=[bj], out=bj.v[:, 0:2 * ext], in0=negc.v[:, 0:2 * ext, h],
                    scalar1=znb.v[:, m, h:h + 1], scalar2=None, op0=ALU.subtract)
                state[(idx, m)] = {"bj": bj, "fo": fop}
            else:
                state[(idx, m)] = {"oa": oacc[i2]}

        def qk(it):
            idx, kind, h, m, st = it
            k_t, v_t, b_t, q_t = loaded[idx]
            n = st // 2; half = st % 2
            ps = nxt(stpB, "st")
            extra_mm = None
            if kind == "a":
                ni = n - 4 * m + 4
                if ni >= 0:
                    extra_mm = (b_t.v[:, ni * 2 + half, :], b_t)
            else:
                ni = n - 4 * m
                if ni >= 0:
                    extra_mm = (fmask.v[:, ni * 2 + half, :], fmask)
            mm(ps.v, k_t.v[:, st * 128:(st + 1) * 128], q_t.v[:, m * 256:(m + 1) * 256], True, extra_mm is None,
               r=[k_t, q_t], w=[ps])
            if extra_mm is not None:
                mm(ps.v, identb.v, extra_mm[0], False, True, r=[identb, extra_mm[1]], w=[ps])
            p_t = ptr_[pc[0] % 8]; pc[0] += 1
            if kind == "a":
                if extra_mm is None:
                    act(p_t.v, ps.v, AF.Exp, r=[ps, rb31], w=[p_t], bias=rb31.v[:, h:h + 1], scale=SCALE)
                else:
                    act(p_t.v, ps.v, AF.Exp, r=[ps], w=[p_t], scale=SCALE)
            else:
                bj = state[(idx, m)]["bj"]
                act(p_t.v, ps.v, AF.Exp, r=[ps, bj], w=[p_t], bias=bj.v[:, st:st + 1], scale=SCALE)
            return p_t

        def finish(idx, kind, h, m, oh):
            hq = h if kind == "a" else 8 + h
            for sub in range(2):
                tr(trB.v[:, sub * 128:(sub + 1) * 128], oh.v[:, sub, :], identb.v, r=[oh, identb], w=[trB])
            os_ = ost[oc_[0] % 2]
            act(os_.v, trB.v[:, 0:256], AF.Copy, r=[trB], w=[os_])
            dma("sp", OTs.ap()[hq, :, m * 256:(m + 1) * 256], os_.v, r=[os_])

        def pv(it, p_t):
            idx, kind, h, m, st = it
            k_t, v_t, b_t, q_t = loaded[idx]
            ext = 4 * m + 4
            n = st // 2; half = st % 2
            last = st == 2 * ext - 1
            stt = state[(idx, m)]
            if kind == "a":
                if half == 0:
                    stt["p0"] = p_t
                else:
                    on = nxt(onp, "on")
                    p0 = stt["p0"]
                    for sub in range(2):
                        mm(on[sub].v, p0.v[:, sub * 128:(sub + 1) * 128], v_t.v[:, st - 1, :], True, False,
                           r=[p0, v_t], w=[on[sub]])
                        mm(on[sub].v, p_t.v[:, sub * 128:(sub + 1) * 128], v_t.v[:, st, :], False, True,
                           r=[p_t, v_t], w=[on[sub]])
                    oa = stt["oa"]
                    for sub in range(2):
                        msk = maskg.v[:, 2 * m + sub, h, n:n + 1]
                        if n == 0:
                            vec("dve", "tensor_scalar", r=[on[sub], maskg], w=[oa], out=oa.v[:, sub, :], in0=on[sub].v,
                                scalar1=msk, scalar2=None, op0=ALU.mult)
                        else:
                            vec("dve", "scalar_tensor_tensor", r=[on[sub], maskg, oa], w=[oa], out=oa.v[:, sub, :],
                                in0=on[sub].v, scalar=msk, in1=oa.v[:, sub, :], op0=ALU.mult, op1=ALU.add)
                if last:
                    oa = stt["oa"]
                    rc_ = rcp[oc_[0] % 2]; oh = ohs[oc_[0] % 2]
                    vec("dve", "reciprocal", r=[oa], w=[rc_], out=rc_.v, in_=oa.v[:, :, 128])
                    for sub in range(2):
                        vec("dve", "tensor_scalar", r=[oa, rc_], w=[oh], out=oh.v[:, sub, :], in0=oa.v[:, sub, 0:128],
                            scalar1=rc_.v[:, sub:sub + 1], scalar2=None, op0=ALU.mult)
                    finish(idx, kind, h, m, oh)
                    oc_[0] += 1
            else:
                fo = stt["fo"]
                for sub in range(2):
                    mm(fo[sub].v, p_t.v[:, sub * 128:(sub + 1) * 128], v_t.v[:, st, :], st == 0, last,
                       r=[p_t, v_t], w=[fo[sub]])
                if last:
                    rc_ = rcp[oc_[0] % 2]; oh = ohs[oc_[0] % 2]
                    for sub in range(2):
                        vec("dve", "reciprocal", r=[fo[sub]], w=[rc_], out=rc_.v[:, sub:sub + 1], in_=fo[sub].v[:, 128:129])
                        vec("dve", "tensor_scalar", r=[fo[sub], rc_], w=[oh], out=oh.v[:, sub, :], in0=fo[sub].v[:, 0:128],
                            scalar1=rc_.v[:, sub:sub + 1], scalar2=None, op0=ALU.mult)
                    finish(idx, kind, h, m, oh)
                    oc_[0] += 1

        LOOK = int(os.environ.get('K_LOOK', '3'))
        pend = []

        def retire():
            it0, p0_ = pend.pop(0)
            pv(it0, p0_)
            idx0, _, _, m0_, st0 = it0
            if m0_ == NBO - 1 and st0 == 2 * (4 * m0_ + 4) - 1 and idx0 + 2 < len(heads):
                load_head(idx0 + 2)

        for it in items:
            if it[4] == 0:
                group_begin(it[0], it[1], it[2], it[3])
            p_t = qk(it)
            pend.append((it, p_t))
            if len(pend) > LOOK:
                retire()
        while pend:
            retire()
        P.barrier()
        A.off = mq
        if stop == "B":
            P.emit(block, sems)
            return nc

        hres = A.alloc([128, 4, 2048], F32)
        aT = A.alloc([128, 16, 512], BF16)
        ot = A.alloc([128, 16, 512], BF16)
        mT = A.alloc([128, 16, 512], BF16)
        wsl = A.ring(3, [128, 8192], BF16)
        sar = A.ring(4, [128, 512], F32)
        sfr = A.ring(4, [128, 512], F32)
        rlr = A.ring(2, [128, 512], F32)
        qcT = A.alloc([128, 4, 512], BF16)
        ptc = A.ring(2, [128, 512], BF16)
        octm = A.alloc([128, 4, 512], BF16)
        ocT = A.alloc([128, 4, 512], BF16)
        memt = []
        for j in range(2):
            tt = T(mT.v[:, j * 8:(j + 1) * 8, :].rearrange("p a b -> p (a b)").bitcast(F32)); tt.b = mT.b
            memt.append(tt)
        hs = [T(hres.v[:, j, :]) for j in range(4)]
        atmC = list(atm)
        for src_t in (octm, ocT):
            t_ = T(src_t.v.rearrange("p a b -> p (a b)")); t_.b = src_t.b
            atmC.append(t_)

        def norm_tile(g):
            outs = []
            for j in range(4):
                outs.append(norm_a(hs[j], g, atmC))
            for j in range(4):
                norm_tr(outs[j], aT, j * 128)

        wc = [0]
        if os.environ.get('K_STATS'):
            print('ARENA phaseC off', A.off, 'of', A.size, 'pers_mark', pers_mark, flush=True)
        gc = [1]

        PRE = 2
        plan = []
        plan.append(([16, 512], [lambda t: load_w(t, w_ck, 0, 16, 0, 512)]))
        plan.append(([16, 512], [lambda t: load_w(t, w_cv, 0, 16, 0, 512)]))
        for _tc in range(NO // 512):
            for fg in range(4):
                plan.append(([16, 512], [lambda t, fg=fg: load_w(t, w_in, 0, 16, C_GA + fg * 512, 512)]))
                plan.append(([16, 512], [lambda t, fg=fg: load_w(t, w_in, 0, 16, C_GF + fg * 512, 512)]))
                plan.append(([16, 512], [lambda t, fg=fg: load_w(t, w_bm, 0, 8, fg * 512, 512, k0=0),
                                         lambda t, fg=fg: load_w(t, w_bf, 0, 8, fg * 512, 512, k0=8)]))
            for cg in range(4):
                plan.append(([16, 512], [lambda t, cg=cg: load_w(t, w_mix, 0, 16, cg * 512, 512)]))
            plan.append(([16, 512], [lambda t: load_w(t, w_cq, 0, 16, 0, 512)]))
            plan.append(([4, 2048], [lambda t: load_w(t, w_co, 0, 4, 0, 2048)]))
            for fg in range(NFG):
                for q4 in range(4):
                    plan.append(([16, 512], [lambda t, fg=fg, q4=q4: load_w(t, w_ff1, 0, 16, fg * 2048 + q4 * 512, 512)]))
                for cg in range(4):
                    plan.append(([16, 512], [lambda t, fg=fg, cg=cg: load_w(t, w_ff2, fg * 2048, 16, cg * 512, 512)]))
        wst = {"issued": 0, "taken": 0, "tiles": {}}

        def wslot(shape3):
            i = wst["taken"]
            while wst["issued"] < min(len(plan), i + 1 + PRE):
                k = wst["issued"]
                shp, loaders = plan[k]
                w = wsl[k % 3]
                t = T(w.v[:, 0:shp[0] * shp[1]].rearrange("p (a b) -> p a b", a=shp[0])); t.b = w.b
                for fn in loaders:
                    fn(t)
                wst["tiles"][k] = t
                wst["issued"] += 1
            assert plan[i][0] == list(shape3), (i, plan[i][0], shape3)
            wst["taken"] += 1
            return wst["tiles"].pop(i)

        def load_g(gd):
            g = gbr[0]
            dma("sp", g.v, gd[:, :], w=[g])
            return g

        def add_res(i, sub, cg, pm):
            vec("dve", "tensor_tensor", r=[pm, hs[sub]], w=[hs[sub]], out=hs[sub].v[:, cg * 512:(cg + 1) * 512], in0=pm.v,
                in1=hs[sub].v[:, cg * 512:(cg + 1) * 512], op=ALU.add)

        gm = load_g(gmem_d)
        vec("pool", "memset", w=[Vc], ap=Vc.v, constant=1.0)
        for j in range(2):
            dma("sp", memt[j].v, memb[j * 128:(j + 1) * 128, :], w=[memt[j]])
            norm_T(memt[j], gm, aT, j * 128)
        wk = wslot([16, 512])
        for hc in range(4):
            pm = nxt(mmp, "mm")
            for dc in range(16):
                mm(pm.v[:, 0:256], wk.v[:, dc, hc * 128:(hc + 1) * 128], aT.v[:, dc, 0:256], dc == 0, dc == 15, r=[wk, aT], w=[pm])
            copy_any(hc, KcT.v[:, hc, :], pm.v[:, 0:256], r=[pm], w=[KcT])
        wv = wslot([16, 512])
        for j in range(2):
            pm = nxt(mmp, "mm")
            for dc in range(16):
                mm(pm.v, aT.v[:, dc, j * 128:(j + 1) * 128], wv.v[:, dc, :], dc == 0, dc == 15, r=[wv, aT], w=[pm])
            copy_any(j, Vc.v[:, j, :, 0:128], pm.v.rearrange("p (a b) -> p a b", a=4), r=[pm], w=[Vc])

        out_dmas = []
        for tc_ in range(NO // 512):
            t0 = tc_ * 512
            g1 = load_g(gmix_d)
            for j in range(4):
                dma("sp", hs[j].v, xo[t0 + j * 128:t0 + (j + 1) * 128, :], w=[hs[j]])
            norm_tile(g1)
            dma("sp", ot.v, OTv[:, :, t0:t0 + 512], w=[ot])
            for fg in range(4):
                wga = wslot([16, 512])
                for f4 in range(4):
                    cs = slice(f4 * 128, (f4 + 1) * 128)
                    pga = nxt(mmp, "mm")
                    for dc in range(16):
                        mm(pga.v, wga.v[:, dc, cs], aT.v[:, dc, :], dc == 0, dc == 15, r=[wga, aT], w=[pga])
                    act(sar[f4].v, pga.v, AF.Sigmoid, r=[pga], w=[sar[f4]])
                wgf = wslot([16, 512])
                for f4 in range(4):
                    cs = slice(f4 * 128, (f4 + 1) * 128)
                    pgf = nxt(mmp, "mm")
                    for dc in range(16):
                        mm(pgf.v, wgf.v[:, dc, cs], aT.v[:, dc, :], dc == 0, dc == 15, r=[wgf, aT], w=[pgf])
                    act(sfr[f4].v, pgf.v, AF.Sigmoid, r=[pgf], w=[sfr[f4]])
                wb = wslot([16, 512])
                for f4 in range(4):
                    fc = fg * 4 + f4
                    cs = slice(f4 * 128, (f4 + 1) * 128)
                    pua = nxt(mmp, "mm")
                    for h in range(8):
                        mm(pua.v, wb.v[:, h, cs], ot.v[:, h, :], h == 0, h == 7, r=[wb, ot], w=[pua])
                    vec("dve", "tensor_tensor", r=[pua, sar[f4]], w=[sar[f4]], out=sar[f4].v, in0=pua.v, in1=sar[f4].v, op=ALU.mult)
                    puf = nxt(mmp, "mm")
                    for h in range(8):
                        mm(puf.v, wb.v[:, 8 + h, cs], ot.v[:, 8 + h, :], h == 0, h == 7, r=[wb, ot], w=[puf])
                    vec("dve", "tensor_tensor", r=[puf, sfr[f4]], w=[sfr[f4]], out=sfr[f4].v, in0=puf.v, in1=sfr[f4].v, op=ALU.mult)
                    vec("pool", "tensor_tensor", r=[sar[f4], sfr[f4]], w=[mT], out=mT.v[:, fc, :], in0=sar[f4].v, in1=sfr[f4].v, op=ALU.add)
            for cg in range(4):
                wm = wslot([16, 512])
                for sub in range(4):
                    pm = nxt(mmp, "mm")
                    for fc in range(16):
                        mm(pm.v, mT.v[:, fc, sub * 128:(sub + 1) * 128], wm.v[:, fc, :], fc == 0, fc == 15, r=[mT, wm], w=[pm])
                    add_res(0, sub, cg, pm)
            g2 = load_g(gcross_d)
            norm_tile(g2)
            wq = wslot([16, 512])
            for hc in range(4):
                pm = nxt(mmp, "mm")
                for dc in range(16):
                    mm(pm.v, wq.v[:, dc, hc * 128:(hc + 1) * 128], aT.v[:, dc, :], dc == 0, dc == 15, r=[wq, aT], w=[pm])
                copy_any(hc, qcT.v[:, hc, :], pm.v, r=[pm], w=[qcT])
            wo = wslot([4, 2048])
            for hc in range(4):
                pts = []
                for st in range(2):
                    ps = nxt(mmp, "mm")
                    mm(ps.v, KcT.v[:, hc, st * 128:(st + 1) * 128], qcT.v[:, hc, :], True, True, r=[KcT, qcT], w=[ps])
                    p_t = ptc[st]
                    act(p_t.v, ps.v, AF.Exp, r=[ps], w=[p_t], scale=SCALE)
                    pts.append(p_t)
                for sub in range(4):
                    po = misc[sub % 2]
                    for st in range(2):
                        mm(po.v[:, 0:129], pts[st].v[:, sub * 128:(sub + 1) * 128], Vc.v[:, st, hc, :], st == 0, st == 1,
                           r=[pts[st], Vc], w=[po])
                    rc_ = rstdr[rc["ss"] % 4]; rc["ss"] += 1
                    vec("dve", "reciprocal", r=[po], w=[rc_], out=rc_.v, in_=po.v[:, 128:129])
                    vec("dve", "tensor_scalar", r=[po, rc_], w=[octm], out=octm.v[:, sub, hc * 128:(hc + 1) * 128], in0=po.v[:, 0:128],
                        scalar1=rc_.v[:, 0:1], scalar2=None, op0=ALU.mult)
            for sub in range(4):
                pt = nxt(trp, "tr")
                for hc in range(4):
                    tr(pt.v[:, hc * 128:(hc + 1) * 128], octm.v[:, sub, hc * 128:(hc + 1) * 128], identb.v, r=[octm, identb], w=[pt])
                copy_any(sub, ocT.v[:, :, sub * 128:(sub + 1) * 128], pt.v[:, 0:512].rearrange("p (a b) -> p a b", a=4),
                         r=[pt], w=[ocT])
            for cg in range(4):
                for sub in range(4):
                    pm = nxt(mmp, "mm")
                    for hc in range(4):
                        mm(pm.v, ocT.v[:, hc, sub * 128:(sub + 1) * 128], wo.v[:, hc, cg * 512:(cg + 1) * 512], hc == 0, hc == 3,
                           r=[ocT, wo], w=[pm])
                    add_res(0, sub, cg, pm)
            g3 = load_g(gmlp_d)
            norm_tile(g3)
            for fg in range(NFG):
                hid = ot if fg % 2 == 0 else mT
                for q4 in range(4):
                    w1 = wslot([16, 512])
                    for f4 in range(4):
                        fc = q4 * 4 + f4
                        pm = nxt(mmp, "mm")
                        for dc in range(16):
                            mm(pm.v, w1.v[:, dc, f4 * 128:(f4 + 1) * 128], aT.v[:, dc, :], dc == 0, dc == 15, r=[w1, aT], w=[pm])
                        rl = rlr[fc % 2]
                        act(rl.v, pm.v, AF.Relu, r=[pm], w=[rl])
                        vec("pool", "tensor_tensor", r=[rl], w=[hid], out=hid.v[:, fc, :], in0=rl.v, in1=rl.v, op=ALU.mult)
                for cg in range(4):
                    w2 = wslot([16, 512])
                    for sub in range(4):
                        pm = nxt(mmp, "mm")
                        for fc in range(16):
                            mm(pm.v, hid.v[:, fc, sub * 128:(sub + 1) * 128], w2.v[:, fc, :], fc == 0, fc == 15, r=[hid, w2], w=[pm])
                        add_res(0, sub, cg, pm)
            g4 = load_g(gfin_d)
            for j in range(4):
                ss = ssr[rc["ss"] % 4]; rstd = rstdr[rc["ss"] % 4]; rc["ss"] += 1
                vec("pool", "memset", w=[ss], ap=ss.v, constant=0.0)
                act(sqj.v, hs[j].v, AF.Square, r=[hs[j]], w=[sqj, ss], accum_out=ss.v)
                act(rstd.v, ss.v, AF.Sqrt, r=[ss], w=[rstd], bias=EPS, scale=1.0 / D)
                vec("dve", "reciprocal", r=[rstd], w=[rstd], out=rstd.v, in_=rstd.v)
                vec("dve", "scalar_tensor_tensor", r=[hs[j], rstd, g4], w=[hs[j]], out=hs[j].v, in0=hs[j].v,
                    scalar=rstd.v[:, 0:1], in1=g4.v, op0=ALU.mult, op1=ALU.mult)
                out_dmas.append(dma("sp", out_d[t0 + j * 128:t0 + (j + 1) * 128, :], hs[j].v, r=[hs[j]]))
        P.add("sp", lambda: nc.sync.nop(), extra=out_dmas)
        P.barrier()
        P.emit(block, sems)
    return nc


def _t5_bucket_table(maxd):
    n = np.arange(maxd + 1)
    nf = np.maximum(n, 1).astype(np.float32)
    ratio = (np.log((nf / np.float32(16)).astype(np.float32)).astype(np.float32)
             / np.float32(math.log(1024 / 16))).astype(np.float32)
    large = 16 + (ratio * np.float32(16)).astype(np.float32).astype(np.int32)
    large = np.minimum(large, 31)
    return np.where(n < 16, n, large)


def _core_tables(r, NB, NBO):
    bt = _t5_bucket_table(8 * 256 + 512)
    oh = np.zeros((33, 8, 512), np.float32)
    u = np.arange(511)
    for ni in range(8):
        npr = ni - 4
        dist = (r - npr) * 256 + (u - 255)
        b = np.where(dist < 0, 32, bt[np.clip(dist, 0, len(bt) - 1)])
        oh[b, ni, u] = 1.0
    fm = np.zeros((128, 4, 2, 256), np.float32)
    s = np.arange(128)[:, None]
    q = np.arange(256)[None, :]
    for npr in range(4):
        for half in range(2):
            if npr < r:
                continue
            if npr > r:
                fm[:, npr, half, :] = NEG
            else:
                fm[:, npr, half, :] = np.where(half * 128 + s > q, NEG, 0.0)
    sel = np.zeros((128, 8, 128), np.float32)
    sel[0, 2 * r + 1, :] = 1.0
    elneg = np.zeros((NBO, NB), np.float32)
    el01 = np.zeros((NBO, NB), np.float32)
    own = np.zeros((NBO, NB), np.float32)
    for m in range(NBO):
        o = 4 * m + r
        el01[m, :o] = 1.0
        elneg[m, o:] = -1e30
        own[m, o] = 1.0
    bc = lambda a: np.ascontiguousarray(np.broadcast_to(a.reshape(1, -1), (128, a.size)))
    return dict(oh33=oh.reshape(33, 4096), fmask=fm.reshape(128, 2048), sel=sel.reshape(128, 1024),
                eligneg=bc(elneg), elig01=bc(el01), own01=bc(own))


def _prep(inputs, S):
    NB = S // 256
    NBO = NB // 4
    f = lambda a: np.ascontiguousarray(np.asarray(a, dtype=np.float32))
    x = f(inputs["x"]); mem = f(inputs["mem"])
    bc = lambda g: np.ascontiguousarray(np.broadcast_to(f(g).reshape(1, D), (128, D)))
    rel = f(inputs["rel_bias"])
    shared = dict(
        w_in=f(inputs["w_in"][0]), w_bm=f(inputs["w_branch_moba"][0]), w_bf=f(inputs["w_branch_fox"][0]),
        w_mix=f(inputs["w_mix_out"][0]), w_cq=f(inputs["w_cq"][0]), w_ck=f(inputs["w_ck"][0]), w_cv=f(inputs["w_cv"][0]),
        w_co=f(inputs["w_co"][0]), w_ff1=f(inputs["w_ff1"][0]), w_ff2=f(inputs["w_ff2"][0]),
        g_mix_b=bc(inputs["g_mix"][0]), g_cross_b=bc(inputs["g_cross"][0]), g_mem_b=bc(inputs["g_mem"][0]),
        g_mlp_b=bc(inputs["g_mlp"][0]), g_final_b=bc(inputs["g_final"]),
        b_forget_c=f(inputs["b_forget"][0]).reshape(8, 1), rbT=np.ascontiguousarray(rel.T),
        rb31b=np.ascontiguousarray(np.broadcast_to(rel[:, 31].reshape(1, 8), (128, 8))),
        ident=np.eye(128, dtype=np.float32),
    )
    tabs = [_core_tables(r, NB, NBO) for r in range(4)]
    in_maps = []
    rows = []
    for c in range(8):
        b, r = c // 4, c % 4
        xbb = x[b]
        xbr = np.ascontiguousarray(xbb.reshape(S // 128, 128, D)[:, ::-1, :].reshape(S, D))
        idx = np.concatenate([np.arange((4 * m + r) * 256, (4 * m + r + 1) * 256) for m in range(NBO)])
        rows.append((b, idx))
        mp = dict(shared)
        mp.update(tabs[r])
        mp.update(xb=xbb, xbr=xbr, xo=np.ascontiguousarray(xbb[idx]), memb=mem[b])
        in_maps.append(mp)
    return in_maps, rows


_NC_CACHE = {}


def _run(inputs, S, DFF, dbg=False, stop=None):
    key = (S, DFF, dbg)
    if key not in _NC_CACHE:
        _NC_CACHE[key] = build(S, DFF, dbg, stop)
    nc = _NC_CACHE[key]
    in_maps, rows = _prep(inputs, S)
    ncr = int(os.environ.get("K_NCORES", "8"))
    res = run_bass_kernel_spmd(nc, in_maps[:ncr], core_ids=list(range(ncr)))
    out = np.zeros((2, S, D), np.float32)
    for c in range(ncr):
        b, idx = rows[c]
        out[b, idx] = res.results[c]["out"]
    if dbg:
        return out, res
    return out


def kernel(**inputs):
    return _run(inputs, 8192, 8192)
```
